# Optimizing a Trainium2 kernel written in Bass

```python
import math
import jax, jax.numpy as jnp
from jax import lax
import numpy as np

D_MODEL = 1024
BATCH = 32
SEQ = 2048
DEPTH = 1
DEC_BATCH = 16
DEC_SEQ = 2048
PAST_LEN = 128

N_MEM = 256
DA_HEADS = 4
DA_DH = D_MODEL // 16
DA_DV = 2 * DA_DH
DA_W = DA_HEADS * DA_DV
CV_W = D_MODEL // 4
CONV_K = 3
MX_HEADS = 4
MX_DH = D_MODEL // 16
MX_W = MX_HEADS * MX_DH
MIX_W = DA_W + CV_W + MX_W
QK_W = 2 * DA_HEADS * DA_DH
SPLITS = (QK_W, QK_W, DA_W, DA_W, CV_W, CV_W, CV_W, CV_W, MX_W, MX_W)
IN_W = sum(SPLITS)
ROPE_THETA = 500000.0
ROT_DIM = DA_DH // 4
Q_BLOCK = 128
ALPHA = (2 * DEPTH) ** 0.25
BETA = (8 * DEPTH) ** -0.25
LN_EPS = 1e-5

kernel_name = "hybrid_diffattn_shortconv_memxattn_encoder"


def layer_norm(x, g, b):
    x32 = x.astype(jnp.float32)
    mu = jnp.mean(x32, axis=-1, keepdims=True)
    var = jnp.mean(jnp.square(x32 - mu), axis=-1, keepdims=True)
    y = (x32 - mu) * lax.rsqrt(var + LN_EPS) * g.astype(jnp.float32) + b.astype(jnp.float32)
    return y.astype(x.dtype)


def rms_norm(x, g):
    x32 = x.astype(jnp.float32)
    y = x32 * lax.rsqrt(jnp.mean(jnp.square(x32), axis=-1, keepdims=True) + LN_EPS) * g.astype(jnp.float32)
    return y


def rope_tables(seq_len):
    inv_freq = ROPE_THETA ** (-jnp.arange(0, ROT_DIM, 2, dtype=jnp.float32) / ROT_DIM)
    ang = jnp.arange(seq_len, dtype=jnp.float32)[:, None] * inv_freq[None, :]
    return jnp.cos(ang), jnp.sin(ang)


def partial_rope(x, cos, sin):
    xr = x[..., :ROT_DIM].astype(jnp.float32)
    xp = x[..., ROT_DIM:]
    half = ROT_DIM // 2
    x1, x2 = xr[..., :half], xr[..., half:]
    c = cos[None, :, None, None, :]
    s = sin[None, :, None, None, :]
    rot = jnp.concatenate([x1 * c - x2 * s, x2 * c + x1 * s], axis=-1).astype(x.dtype)
    return jnp.concatenate([rot, xp], axis=-1)


def diff_attention(q, k, v, lam, lam_init, subln_g):
    B, S = q.shape[0], q.shape[1]
    nb = S // Q_BLOCK
    scale = DA_DH ** -0.5
    qb = jnp.moveaxis(q.reshape(B, nb, Q_BLOCK, DA_HEADS, 2, DA_DH), 1, 0)

    def block(qblk):
        s = jnp.einsum('bqhcd,bkhcd->bhcqk', qblk, k).astype(jnp.float32) * scale
        p = jax.nn.softmax(s, axis=-1)
        a = p[:, :, 0] - lam * p[:, :, 1]
        return jnp.einsum('bhqk,bkhd->bqhd', a.astype(v.dtype), v)

    o = lax.map(block, qb)
    o = jnp.moveaxis(o, 0, 1).reshape(B, S, DA_HEADS, DA_DV)
    o = rms_norm(o, subln_g) * (1.0 - lam_init)
    return o.reshape(B, S, DA_W).astype(v.dtype)


def short_conv(u, w):
    C = u.shape[-1]
    return lax.conv_general_dilated(
        u, w[:, None, :].astype(u.dtype), window_strides=(1,),
        padding=((CONV_K // 2, CONV_K // 2),),
        dimension_numbers=('NWC', 'WIO', 'NWC'), feature_group_count=C)


def memory_attention(q, mem, w_mem_kv):
    B, S = q.shape[0], q.shape[1]
    kv = mem @ w_mem_kv
    k, v = jnp.split(kv, 2, axis=-1)
    qh = q.reshape(B, S, MX_HEADS, MX_DH)
    kh = k.reshape(B, N_MEM, MX_HEADS, MX_DH)
    vh = v.reshape(B, N_MEM, MX_HEADS, MX_DH)
    s = jnp.einsum('bqhd,bkhd->bhqk', qh, kh).astype(jnp.float32) * (MX_DH ** -0.5)
    p = jax.nn.softmax(s, axis=-1).astype(v.dtype)
    o = jnp.einsum('bhqk,bkhd->bqhd', p, vh)
    return o.reshape(B, S, MX_W)


def encoder_layer(x, mem, w_in, w_mem_kv, lam_q1, lam_k1, lam_q2, lam_k2, subln_g,
                  conv_w, w_o, ln_g, ln_b, cos, sin, layer_idx):
    B, S, _ = x.shape
    proj = x @ w_in
    idx = []
    acc = 0
    for w in SPLITS[:-1]:
        acc += w
        idx.append(acc)
    q_da, k_da, v_da, g_da, b_cv, c_cv, h_cv, g_cv, q_mx, g_mx = jnp.split(proj, idx, axis=-1)

    lam_init = 0.8 - 0.6 * math.exp(-0.3 * layer_idx)
    lam = (jnp.exp(jnp.sum(lam_q1.astype(jnp.float32) * lam_k1.astype(jnp.float32)))
           - jnp.exp(jnp.sum(lam_q2.astype(jnp.float32) * lam_k2.astype(jnp.float32))) + lam_init)
    q = partial_rope(q_da.reshape(B, S, DA_HEADS, 2, DA_DH), cos, sin)
    k = partial_rope(k_da.reshape(B, S, DA_HEADS, 2, DA_DH), cos, sin)
    v = v_da.reshape(B, S, DA_HEADS, DA_DV)
    o_da = diff_attention(q, k, v, lam, lam_init, subln_g) * jax.nn.silu(g_da)

    o_cv = b_cv * short_conv(c_cv * h_cv, conv_w) * jax.nn.silu(g_cv)

    o_mx = memory_attention(q_mx, mem, w_mem_kv) * jax.nn.silu(g_mx)

    out = jnp.concatenate([o_da, o_cv, o_mx], axis=-1) @ w_o
    return layer_norm(ALPHA * x + out, ln_g, ln_b)


def trunk(x, mem, in_ln_g, in_ln_b, w_in, w_mem_kv, lam_q1, lam_k1, lam_q2, lam_k2,
          subln_g, conv_w, w_o, ln_g, ln_b):
    cos, sin = rope_tables(x.shape[1])
    h = layer_norm(x, in_ln_g, in_ln_b)
    for l in range(DEPTH):
        h = encoder_layer(h, mem, w_in[l], w_mem_kv[l], lam_q1[l], lam_k1[l], lam_q2[l], lam_k2[l],
                          subln_g[l], conv_w[l], w_o[l], ln_g[l], ln_b[l], cos, sin, l)
    return h


def setup_inputs(seed: int = 0) -> dict:
    key = jax.random.key(seed)
    ks = jax.random.split(key, 17)
    f32 = jnp.float32
    v_start = 2 * QK_W
    col_scale = jnp.ones((IN_W,), f32).at[v_start:v_start + DA_W].set(BETA)
    w_in = jax.random.normal(ks[4], (DEPTH, D_MODEL, IN_W), f32) * (D_MODEL ** -0.5) * col_scale
    kv_scale = jnp.ones((2 * MX_W,), f32).at[MX_W:].set(BETA)
    w_mem_kv = jax.random.normal(ks[5], (DEPTH, D_MODEL, 2 * MX_W), f32) * (D_MODEL ** -0.5) * kv_scale
    return {
        "x_prompt": jax.random.normal(ks[0], (BATCH, SEQ, D_MODEL), f32),
        "x_sample": jax.random.normal(ks[1], (DEC_BATCH, DEC_SEQ, D_MODEL), f32),
        "mem_prompt": jax.random.normal(ks[2], (BATCH, N_MEM, D_MODEL), f32),
        "mem_sample": jax.random.normal(ks[3], (DEC_BATCH, N_MEM, D_MODEL), f32),
        "in_ln_g": 1.0 + 0.02 * jax.random.normal(ks[6], (D_MODEL,), f32),
        "in_ln_b": 0.02 * jax.random.normal(ks[7], (D_MODEL,), f32),
        "w_in": w_in,
        "w_mem_kv": w_mem_kv,
        "lam_q1": 0.1 * jax.random.normal(ks[8], (DEPTH, DA_DH), f32),
        "lam_k1": 0.1 * jax.random.normal(ks[9], (DEPTH, DA_DH), f32),
        "lam_q2": 0.1 * jax.random.normal(ks[10], (DEPTH, DA_DH), f32),
        "lam_k2": 0.1 * jax.random.normal(ks[11], (DEPTH, DA_DH), f32),
        "subln_g": 1.0 + 0.02 * jax.random.normal(ks[12], (DEPTH, DA_DV), f32),
        "conv_w": jax.random.normal(ks[13], (DEPTH, CONV_K, CV_W), f32) * (CONV_K ** -0.5),
        "w_o": jax.random.normal(ks[14], (DEPTH, MIX_W, D_MODEL), f32) * (MIX_W ** -0.5) * BETA,
        "ln_g": 1.0 + 0.02 * jax.random.normal(ks[15], (DEPTH, D_MODEL), f32),
        "ln_b": 0.02 * jax.random.normal(ks[16], (DEPTH, D_MODEL), f32),
    }


def reference(x_prompt, x_sample, mem_prompt, mem_sample, in_ln_g, in_ln_b, w_in, w_mem_kv,
              lam_q1, lam_k1, lam_q2, lam_k2, subln_g, conv_w, w_o, ln_g, ln_b):
    y_prompt = trunk(x_prompt, mem_prompt, in_ln_g, in_ln_b, w_in, w_mem_kv, lam_q1, lam_k1,
                     lam_q2, lam_k2, subln_g, conv_w, w_o, ln_g, ln_b)
    y_sample = trunk(x_sample, mem_sample, in_ln_g, in_ln_b, w_in, w_mem_kv, lam_q1, lam_k1,
                     lam_q2, lam_k2, subln_g, conv_w, w_o, ln_g, ln_b)
    return (y_prompt, y_sample)
```

```python
import contextlib
import math
import numpy as np
import concourse.bass as bass
import concourse.mybir as mybir
from concourse.bass_utils import run_bass_kernel_spmd

F32 = mybir.dt.float32
BF16 = mybir.dt.bfloat16
AF = mybir.ActivationFunctionType
ALU = mybir.AluOpType
AX = mybir.AxisListType

D = 1024
S = 2048
NT = 16
NB = 4
NMEM = 256
NCORES = 8
SEQ_PER_CORE = 6
NG = 12
LN_EPS = 1e-5
ALPHA = 2.0 ** 0.25
LAM_INIT = 0.2
ROPE_THETA = 500000.0
CELL = 64


class Op:
    __slots__ = ("eng", "fn", "deps", "idx", "sig", "cnt", "sem", "dma")

    def __init__(self, eng, fn, sem):
        self.eng = eng
        self.fn = fn
        self.deps = set()
        self.sig = False
        self.cnt = 0
        self.sem = sem
        self.dma = sem is not None


class Sched:
    def __init__(self):
        self.engs = {"pe": [], "act": [], "dve": [], "pool": [], "sp": []}
        self.cells = {}
        self.dma_counts = {}

    @staticmethod
    def _cells_of(ap):
        name = ap.tensor.name
        if name not in ("arena", "ps"):
            return ()
        esz = 2 if ap.dtype == BF16 else 4
        pairs = ap.ap
        row = pairs[0][0]
        start = int(ap.offset) % row if row > 0 else int(ap.offset)
        ext = 1
        for st, cn in pairs[1:]:
            ext += (cn - 1) * st
        b0 = start * esz
        b1 = (start + ext) * esz
        c0 = b0 // (CELL * 4)
        c1 = (b1 - 1) // (CELL * 4)
        return [(name, c) for c in range(c0, c1 + 1)]

    def _dep(self, o, p, kind):
        if p is o:
            return
        if p.dma:
            o.deps.add(p)
            return
        if p.eng == o.eng:
            if o.dma:
                o.deps.add(p)
                return
            if o.eng == "pe":
                return
            o.deps.add(p)
            return
        o.deps.add(p)

    def op(self, eng, fn, outs=(), ins=(), xr=(), xw=(), sem=None):
        o = Op(eng, fn, sem)
        o.idx = len(self.engs[eng])
        self.engs[eng].append(o)
        rc = list(xr)
        wc = list(xw)
        for a in ins:
            rc.extend(self._cells_of(a))
        for a in outs:
            wc.extend(self._cells_of(a))
        cells = self.cells
        for r in rc:
            c = cells.get(r)
            if c is not None and c[0] is not None:
                self._dep(o, c[0], "raw")
        for r in wc:
            c = cells.get(r)
            if c is not None:
                if c[0] is not None:
                    self._dep(o, c[0], "waw")
                for rd in c[1].values():
                    self._dep(o, rd, "war")
        key = ("dma", id(o)) if o.dma else eng
        for r in rc:
            c = cells.get(r)
            if c is None:
                c = cells[r] = [None, {}]
            c[1][key] = o
        for r in wc:
            cells[r] = [o, {}]
        if o.dma:
            n = self.dma_counts.get(sem, 0) + 16
            self.dma_counts[sem] = n
            o.cnt = n
        return o

    def finalize(self):
        for ops in self.engs.values():
            for o in ops:
                for p in o.deps:
                    if not p.dma:
                        p.sig = True
        for ops in self.engs.values():
            n = 0
            for o in ops:
                if not o.dma and o.sig:
                    n += 1
                    o.cnt = n

    def emit(self, eng_name, eng, sems):
        waited = {}
        for o in self.engs[eng_name]:
            need = {}
            for p in o.deps:
                key = p.sem if p.dma else ("eng", p.eng)
                if waited.get(key, 0) >= p.cnt:
                    continue
                if need.get(key, 0) < p.cnt:
                    need[key] = p.cnt
            for key, cnt in need.items():
                eng.wait_ge(sems[key], cnt)
                waited[key] = cnt
            inst = o.fn(eng)
            if o.dma:
                inst.then_inc(sems[o.sem], 16)
            elif o.sig:
                inst.then_inc(sems[("eng", o.eng)], 1)


def build_program(nseq):
    nc = bass.Bass("TRN2", target_bir_lowering=False)
    x_d = nc.dram_tensor("x", [nseq, S, D], F32, kind="ExternalInput").ap()
    mem_d = nc.dram_tensor("mem", [nseq, NMEM, D], F32, kind="ExternalInput").ap()
    wsrc_d = nc.dram_tensor("wsrc", [NG, 128, 4096], F32, kind="ExternalInput").ap()
    lnp_d = nc.dram_tensor("lnp", [4, D], F32, kind="ExternalInput").ap()
    ctab_d = nc.dram_tensor("ctab", [128, S], F32, kind="ExternalInput").ap()
    stab_d = nc.dram_tensor("stab", [128, S], F32, kind="ExternalInput").ap()
    ident_d = nc.dram_tensor("ident", [128, 128], F32, kind="ExternalInput").ap()
    permm_d = nc.dram_tensor("permm", [128, 128], F32, kind="ExternalInput").ap()
    cw_d = nc.dram_tensor("cw", [128, 6], F32, kind="ExternalInput").ap()
    sg_d = nc.dram_tensor("subg", [128], F32, kind="ExternalInput").ap()
    lam_d = nc.dram_tensor("lamv", [256], F32, kind="ExternalInput").ap()
    lnfm_d = nc.dram_tensor("lnfm", [128, 16], F32, kind="ExternalInput").ap()
    y_d = nc.dram_tensor("y", [nseq, S, D], F32, kind="ExternalOutput").ap()
    wsc_d = nc.dram_tensor("wsc", [NG, 128, 4096], BF16, kind="Internal").ap()
    rsc_d = nc.dram_tensor("rsc", [2, 512], F32, kind="Internal").ap()

    off = [0]

    def alloc(words):
        words = (words + CELL - 1) // CELL * CELL
        o = off[0]
        off[0] += words
        return o

    o_gin, o_bin, o_g2, o_b2 = alloc(1024), alloc(1024), alloc(1024), alloc(1024)
    o_ctab, o_stab = alloc(2048), alloc(2048)
    o_idb = alloc(64)
    o_ones = alloc(64)
    o_permb = alloc(64)
    o_sgp1 = alloc(1)
    o_lnfm = alloc(16)
    o_sgp = alloc(128)
    o_cw = alloc(8)
    o_lamv = alloc(256)
    o_lprod = alloc(128)
    o_ls = alloc(2)
    o_le = alloc(2)
    o_nl = alloc(1)
    o_stats = alloc(32)
    o_st12 = alloc(12)
    o_mv = alloc(2)
    o_st12b = alloc(12)
    o_mvb = alloc(2)
    o_st2 = alloc(2)
    o_rs = alloc(8)
    o_rsl = alloc(4)
    o_ss = alloc(4)
    o_rinv = alloc(4)
    o_rsm = alloc(16)
    o_mhalf = alloc(4)
    o_smin = alloc(4)
    o_smout = alloc(4)
    o_epsc = alloc(4)
    o_eps128 = alloc(4)
    o_tmpv = alloc(4)
    o_wg = [alloc(2048), alloc(2048)]
    o_QT = alloc(4096)
    o_KT = alloc(4096)
    o_KT1 = alloc(4096)
    o_VA = alloc(4160)
    o_gate = alloc(4096)
    o_ocv = alloc(2048)
    o_omx = alloc(2048)
    o_kmx = alloc(256)
    o_vmx = alloc(264)
    shared0 = off[0]
    o_hT = alloc(8192)
    tmp0 = off[0]
    o_xt = [alloc(1024), alloc(1024), alloc(1024)]
    o_hb = [alloc(512), alloc(512)]
    o_evt = alloc(128)
    endA = off[0]
    off[0] = tmp0
    o_memb = alloc(1024)
    o_memT = alloc(1024)
    endM = off[0]
    off[0] = tmp0
    o_tA, o_tB = alloc(512), alloc(512)
    o_qtmp = [alloc(256) for _ in range(4)]
    endR = off[0]
    off[0] = tmp0
    o_csb, o_sgc, o_bgb, o_u, o_yc = alloc(512), alloc(512), alloc(512), alloc(2052), alloc(512)
    endC = off[0]
    off[0] = tmp0
    o_gtmp = alloc(512)
    off[0] = tmp0
    o_qmxb = alloc(512)
    o_gmxb = alloc(1024)
    o_Em = [alloc(256) for _ in range(4)]
    o_omxf = alloc(1024)
    o_ofm = alloc(512)
    endG8 = off[0]
    off[0] = shared0
    o_E = [alloc(256) for _ in range(4)]
    o_mix = [alloc(1024), alloc(1024)]
    o_rcp = [alloc(512), alloc(512)]
    o_tO = [alloc(512), alloc(512)]
    o_Of = alloc(512)
    o_rscr = alloc(512)
    o_pp = alloc(512)
    o_sqb = alloc(256)
    o_xr = [alloc(1024) for _ in range(4)]
    endP = off[0]
    AW = max(endA, endM, endR, endC, endG8, endP)

    arena = nc.alloc_sbuf_tensor("arena", [128, AW], F32)
    ps = nc.alloc_psum_tensor("ps", [128, 4096], F32)

    def f32v(o, n):
        return arena[:, o:o + n]

    def bfv(o, nwords):
        return arena[:, o:o + nwords].bitcast(BF16)

    def bank(b, n=512):
        return ps[:, b * 512:b * 512 + n]

    def bank_bf(b):
        return ps[:, b * 512:(b + 1) * 512].bitcast(BF16)

    gin, bin_, g2, b2 = f32v(o_gin, 1024), f32v(o_bin, 1024), f32v(o_g2, 1024), f32v(o_b2, 1024)
    ctab, stab = f32v(o_ctab, 2048), f32v(o_stab, 2048)
    idb = bfv(o_idb, 64)
    onesb = bfv(o_ones, 64)
    permb = bfv(o_permb, 64)
    sgp1 = f32v(o_sgp1, 1)
    lnfm = f32v(o_lnfm, 16)
    sgp = f32v(o_sgp, 128)
    cw = f32v(o_cw, 6)
    lamv = f32v(o_lamv, 256)
    lprod = f32v(o_lprod, 128)
    ls, le, nl = f32v(o_ls, 2), f32v(o_le, 2), f32v(o_nl, 1)
    stats = f32v(o_stats, 32).rearrange("p (t k) -> p t k", k=2)
    st12, mv = f32v(o_st12, 12), f32v(o_mv, 2)
    st12b, mvb, st2 = f32v(o_st12b, 12), f32v(o_mvb, 2), f32v(o_st2, 2)
    rs = f32v(o_rs, 8).rearrange("p (q c) -> p q c", c=2)
    rsl, ss, rinv = f32v(o_rsl, 4), f32v(o_ss, 4), f32v(o_rinv, 4)
    rsm = f32v(o_rsm, 16).rearrange("p (q h) -> p q h", h=4)
    sm_in, sm_out = f32v(o_smin, 4), f32v(o_smout, 4)
    mhalf, epsc, eps128, tmpv = f32v(o_mhalf, 4), f32v(o_epsc, 4), f32v(o_eps128, 4), f32v(o_tmpv, 4)
    wg = [bfv(o, 2048).rearrange("p (c n) -> p c n", n=512) for o in o_wg]
    wg_flat = [bfv(o, 2048) for o in o_wg]
    QT = bfv(o_QT, 4096).rearrange("p (h t) -> p h t", t=S)
    KT = bfv(o_KT, 4096).rearrange("p (h t) -> p h t", t=S)
    KT1 = bfv(o_KT1, 4096).rearrange("p (h t) -> p h t", t=S)
    VA = bfv(o_VA, 4160).rearrange("p (t h e) -> p t h e", h=4, e=130)
    gateT = bfv(o_gate, 4096).rearrange("p (h t) -> p h t", t=S)
    ocvT = bfv(o_ocv, 2048).rearrange("p (i t) -> p i t", t=S)
    omxT = bfv(o_omx, 2048).rearrange("p (i t) -> p i t", t=S)
    KmxT = bfv(o_kmx, 256).rearrange("p (i t) -> p i t", t=256)
    Vmx = bfv(o_vmx, 264).rearrange("p (t h e) -> p t h e", h=4, e=66)
    hT = bfv(o_hT, 8192).rearrange("p (b c t) -> p b c t", c=8, t=512)
    xt = [f32v(o, 1024) for o in o_xt]
    hb = [bfv(o, 512) for o in o_hb]
    evt = f32v(o_evt, 128)
    memb = bfv(o_memb, 1024).rearrange("p (t d) -> p t d", d=1024)
    memT = bfv(o_memT, 1024).rearrange("p (c t) -> p c t", t=256)
    tA, tB = f32v(o_tA, 512), f32v(o_tB, 512)
    qtmp = [bfv(o, 256) for o in o_qtmp]
    csb, sgc, bgb, yc = f32v(o_csb, 512), f32v(o_sgc, 512), f32v(o_bgb, 512), f32v(o_yc, 512)
    u = f32v(o_u, 2050)
    gtmp = f32v(o_gtmp, 512)
    qmxb = bfv(o_qmxb, 512).rearrange("p (i t) -> p i t", t=512)
    gmxb = f32v(o_gmxb, 1024).rearrange("p (q n) -> p q n", n=256)
    Em = [bfv(o, 256) for o in o_Em]
    omxf = f32v(o_omxf, 1024)
    ofm = bfv(o_ofm, 512).rearrange("p (q n) -> p q n", n=256)
    E = [bfv(o, 256) for o in o_E]
    mix = [bfv(o, 1024).rearrange("p (h t) -> p h t", t=512) for o in o_mix]
    rcp = [f32v(o, 512) for o in o_rcp]
    tO = [f32v(o, 512) for o in o_tO]
    Of = f32v(o_Of, 512)
    rscr = f32v(o_rscr, 512)
    pp = f32v(o_pp, 512)
    sqb = bfv(o_sqb, 256)
    xr = [f32v(o, 1024) for o in o_xr]

    sc = Sched()
    semkeys = [("eng", e) for e in ("pe", "act", "dve", "pool")]
    dma_sem_names = set()

    def dsem(name):
        dma_sem_names.add(name)
        return name

    def dma_sp(out, in_, sem, xr_=(), xw_=()):
        sc.op("sp", lambda e: e.dma_start(out=out, in_=in_), outs=[out], ins=[in_],
              xr=xr_, xw=xw_, sem=dsem(sem))

    def dma_pool(out, in_, sem, xr_=(), xw_=()):
        sc.op("pool", lambda e: e.dma_start(out=out, in_=in_), outs=[out], ins=[in_],
              xr=xr_, xw=xw_, sem=dsem(sem))

    def mm(out, lhsT, rhs, start, stop, skip=False):
        sc.op("pe", lambda e: e.matmul(out, lhsT, rhs, start=start, stop=stop, skip_group_check=skip),
              outs=[out], ins=[lhsT, rhs])

    def tr(out, in_):
        sc.op("pe", lambda e: e.transpose(out, in_, idb), outs=[out], ins=[in_, idb])

    def act(out, in_, func, bias=0.0, scale=1.0, extra=()):
        ins = [in_] + [a for a in (bias, scale) if not isinstance(a, float)] + list(extra)
        sc.op("act", lambda e: e.activation(out, in_, func, bias=bias, scale=scale),
              outs=[out], ins=ins)

    def tt(eng, out, in0, in1, op, extra=()):
        sc.op(eng, lambda e: e.tensor_tensor(out, in0, in1, op), outs=[out], ins=[in0, in1] + list(extra))

    def ts(eng, out, in0, s1, s2, op0, op1=None):
        ins = [in0] + [a for a in (s1, s2) if a is not None and not isinstance(a, float)]
        if op1 is None:
            sc.op(eng, lambda e: e.tensor_scalar(out, in0, s1, s2, op0), outs=[out], ins=ins)
        else:
            sc.op(eng, lambda e: e.tensor_scalar(out, in0, s1, s2, op0, op1), outs=[out], ins=ins)

    def stt(eng, out, in0, scalar, in1, op0, op1):
        ins = [in0, in1] + ([scalar] if not isinstance(scalar, float) else [])
        sc.op(eng, lambda e: e.scalar_tensor_tensor(out, in0, scalar, in1, op0, op1),
              outs=[out], ins=ins)

    def cp(eng, out, in_):
        if eng == "act":
            act(out, in_, AF.Copy)
        else:
            sc.op(eng, lambda e: e.tensor_copy(out, in_), outs=[out], ins=[in_])

    def memset(eng, out, val):
        sc.op(eng, lambda e: e.memset(out, val), outs=[out])

    def bc(ap2d, shape):
        return ap2d.unsqueeze(len(ap2d.shape)).broadcast_to(shape)

    for i, dst in enumerate((gin, bin_, g2, b2)):
        dma_sp(dst, lnp_d[i].partition_broadcast(128), "c%d" % i)
    dma_sp(ctab, ctab_d, "c4")
    dma_sp(stab, stab_d, "c5")
    dma_sp(cw, cw_d, "c6")
    dma_sp(sgp, sg_d.partition_broadcast(128), "c7")
    dma_sp(sgp1, sg_d.rearrange("(p o) -> p o", o=1), "c10")
    dma_sp(lnfm, lnfm_d, "c11")
    dma_sp(lamv, lam_d.partition_broadcast(128), "c8")
    dma_pool(idb, ident_d, "c9")
    dma_pool(permb, permm_d, "c12")
    ts("dve", sgp1, sgp1, (1.0 - LAM_INIT) * math.sqrt(128.0), None, ALU.mult)
    memset("pool", onesb, 1.0)
    lv = lamv.rearrange("p (a b d) -> p a b d", a=2, b=2)
    tt("dve", lprod.rearrange("p (a d) -> p a d", d=64), lv[:, :, 0, :], lv[:, :, 1, :], ALU.mult)
    sc.op("dve", lambda e: e.tensor_reduce(ls, lprod.rearrange("p (a d) -> p a d", d=64), AX.X, ALU.add),
          outs=[ls], ins=[lprod])
    act(le, ls, AF.Exp)
    stt("dve", nl, le[:, 0:1], -1.0, le[:, 1:2], ALU.mult, ALU.add)
    ts("dve", nl, nl, -LAM_INIT, None, ALU.add)
    memset("pool", mhalf, -0.5)
    memset("pool", epsc, LN_EPS)
    memset("pool", eps128, 128.0 * LN_EPS)
    memset("pool", KT[64:128, :, :], 0.0)
    memset("pool", KT1[0:64, :, :], 0.0)
    memset("pool", VA[:, :, :, 128:130], 1.0)
    memset("pool", Vmx[:, :, :, 64:66], 1.0)
    for g in range(NG):
        sl = g % 2
        dma_pool(wg_flat[sl], wsrc_d[g], "wc%d" % sl)
        dma_sp(wsc_d[g], wg_flat[sl], "ws%d" % sl, xw_=[("wsc", g)])

    wslot = [0]

    def load_w(g):
        sl = wslot[0] % 2
        wslot[0] += 1
        dma_sp(wg_flat[sl], wsc_d[g], "wl%d" % sl, xr_=[("wsc", g)])
        return sl

    psrot = [0]

    def next_pair():
        p = psrot[0] % 2
        psrot[0] += 1
        return 2 * p, 2 * p + 1

    def fm_tile(bk, sl, i, blk):
        for c in range(8):
            mm(bank(bk), wg[sl][:, c, i * 128:(i + 1) * 128], hT[:, blk, c, :], c == 0, c == 7)

    def ln_stats(src, st_, mv_, rstd_out, nmr_out):
        sc.op("dve", lambda e: e.bn_stats(st_[:, 0:6], src[:, 0:512]), outs=[st_[:, 0:6]], ins=[src[:, 0:512]])
        sc.op("dve", lambda e: e.bn_stats(st_[:, 6:12], src[:, 512:1024]), outs=[st_[:, 6:12]], ins=[src[:, 512:1024]])
        sc.op("dve", lambda e: e.bn_aggr(mv_, st_), outs=[mv_], ins=[st_])
        tt("pool", tmpv[:, 0:1], mv_[:, 1:2], epsc[:, 0:1], ALU.add)
        tt("pool", rstd_out, tmpv[:, 0:1], mhalf[:, 0:1], ALU.pow)
        stt("dve", nmr_out, mv_[:, 0:1], -1.0, rstd_out, ALU.mult, ALU.mult)

    ystores = {}
    ACC = ps[:, 4 * 512:8 * 512].rearrange("p (q n) -> p q n", n=512)

    for s in range(nseq):
        slkv = load_w(9)
        dma_pool(memb, mem_d[s].rearrange("(t p) d -> p t d", p=128), "mem")
        for t in range(2):
            pt = bank_bf(7)
            for c in range(8):
                tr(pt[:, c * 128:(c + 1) * 128], memb[:, t, c * 128:(c + 1) * 128])
            cp("act", memT[:, :, t * 128:(t + 1) * 128], pt.rearrange("p (c t) -> p c t", t=128))
        for i in range(2):
            bk = next_pair()[0]
            for c in range(8):
                mm(bank(bk, 256), wg[slkv][:, c, i * 128:(i + 1) * 128], memT[:, c, :], c == 0, c == 7)
            cp("act", KmxT[:, i, :], bank(bk, 256))
        for t in range(2):
            bk = next_pair()[0]
            for c in range(8):
                mm(bank(bk, 256), memT[:, c, t * 128:(t + 1) * 128], wg[slkv][:, c, 256:512], c == 0, c == 7)
            cp("dve", Vmx[:, t, :, 0:64], bank(bk, 256).rearrange("p (h e) -> p h e", e=64))

        sl_next = load_w(0)
        for t in range(NT):
            xs = xt[t % 3]
            hbt = hb[t % 2]
            dma_sp(xs, x_d[s, t * 128:(t + 1) * 128, :], "x%d" % (t % 3))
            ln_stats(xs, st12, mv, stats[:, t, 0:1], stats[:, t, 1:2])
            act(hbt, xs, AF.Identity, bias=stats[:, t, 1:2], scale=stats[:, t, 0:1])
            pt = bank_bf(6 + (t % 2))
            for c in range(8):
                tr(pt[:, c * 128:(c + 1) * 128], hbt[:, c * 128:(c + 1) * 128])
            for c in range(8):
                dst = hT[:, t // 4, c, (t % 4) * 128:(t % 4 + 1) * 128]
                if t % 2 == 0:
                    act(dst, pt[:, c * 128:(c + 1) * 128],
                        AF.Identity, bias=lnfm[:, 8 + c:9 + c], scale=lnfm[:, c:c + 1], extra=[pt])
                else:
                    sc.op("dve", lambda e, c=c, pt=pt: e.tensor_scalar(evt, pt[:, c * 128:(c + 1) * 128], lnfm[:, c:c + 1], None, ALU.mult),
                          outs=[evt], ins=[pt, lnfm[:, c:c + 1]])
                    ts("dve", dst, evt, lnfm[:, 8 + c:9 + c], None, ALU.add)

        for g in range(9):
            sl = sl_next
            if g < 8:
                sl_next = load_w(g + 1)
            if g < 4:
                h = g
                for blk in range(NB + 1):
                    if blk < NB:
                        sb = 4 * (blk % 2)
                        fm_tile(sb, sl, 0, blk)
                        fm_tile(sb + 2, sl, 2, blk)
                        cp("act", qtmp[2 * (blk % 2)], bank(sb))
                        cp("act", qtmp[2 * (blk % 2) + 1], bank(sb + 2))
                    if blk >= 1:
                        pb_ = blk - 1
                        sb = 4 * (pb_ % 2)
                        cs = slice(pb_ * 512, (pb_ + 1) * 512)
                        mm(bank(sb + 1), permb, qtmp[2 * (pb_ % 2)], True, True)
                        mm(bank(sb + 3), permb, qtmp[2 * (pb_ % 2) + 1], True, True)
                        tt("dve", tA, bank(sb), ctab[:, cs], ALU.mult, extra=[qtmp[2 * (pb_ % 2)]])
                        tt("dve", tB, bank(sb + 1), stab[:, cs], ALU.mult)
                        tt("dve", QT[:, h, cs], tA, tB, ALU.add)
                        tt("dve", tA, bank(sb + 2), ctab[:, cs], ALU.mult, extra=[qtmp[2 * (pb_ % 2) + 1]])
                        tt("dve", tB, bank(sb + 3), stab[:, cs], ALU.mult)
                        tt("dve", KT[0:64, h, cs], tA[0:64, :], tB[0:64, :], ALU.add)
                        tt("dve", KT1[64:128, h, cs], tA[64:128, :], tB[64:128, :], ALU.add)
            elif g < 6:
                i = g - 4
                memset("pool", u[:, 0:1], 0.0)
                memset("pool", u[:, 2049:2050], 0.0)
                for blk in range(NB):
                    ba, bb = next_pair()
                    fm_tile(ba, sl, 0, blk)
                    cp("act", csb, bank(ba))
                    fm_tile(bb, sl, 1, blk)
                    tt("dve", u[:, 1 + blk * 512:1 + (blk + 1) * 512], bank(bb), csb, ALU.mult)
                for blk in range(NB):
                    ba, bb = next_pair()
                    fm_tile(ba, sl, 2, blk)
                    act(sgc, bank(ba), AF.Silu)
                    fm_tile(bb, sl, 3, blk)
                    tt("dve", bgb, bank(bb), sgc, ALU.mult)
                    b0 = blk * 512
                    ts("dve", yc, u[:, b0:b0 + 512], cw[:, 3 * i:3 * i + 1], None, ALU.mult)
                    stt("dve", yc, u[:, b0 + 1:b0 + 513], cw[:, 3 * i + 1:3 * i + 2], yc, ALU.mult, ALU.add)
                    stt("dve", yc, u[:, b0 + 2:b0 + 514], cw[:, 3 * i + 2:3 * i + 3], yc, ALU.mult, ALU.add)
                    tt("dve", ocvT[:, i, b0:b0 + 512], yc, bgb, ALU.mult)
            elif g == 6:
                for t in range(NT):
                    bk = next_pair()[t % 2]
                    for c in range(8):
                        mm(bank(bk), hT[:, t // 4, c, (t % 4) * 128:(t % 4 + 1) * 128], wg[sl][:, c, :], c == 0, c == 7)
                    cp("act", VA[:, t, :, 0:128], bank(bk).rearrange("p (h e) -> p h e", e=128))
            elif g == 7:
                for i in range(4):
                    for blk in range(NB):
                        bk = next_pair()[blk % 2]
                        fm_tile(bk, sl, i, blk)
                        act(gateT[:, i, blk * 512:(blk + 1) * 512], bank(bk), AF.Silu)
            else:
                for blk in range(NB):
                    for i in range(2):
                        bk = next_pair()[i]
                        fm_tile(bk, sl, i, blk)
                        cp("dve", qmxb[:, i, :], bank(bk))
                    for q in range(4):
                        bk = next_pair()[q % 2]
                        for c in range(8):
                            mm(bank(bk, 256), hT[:, blk, c, q * 128:(q + 1) * 128], wg[sl][:, c, 256:512], c == 0, c == 7)
                        act(gmxb[:, q, :], bank(bk, 256), AF.Silu)
                    its = [(p, mk) for p in range(2) for mk in range(2)]

                    def mx_s(n):
                        p, mk = its[n]
                        b0 = 2 * (n % 2)
                        mm(bank(b0), KmxT[0:64, p, mk * 128:(mk + 1) * 128], qmxb[0:64, p, :], True, True)
                        mm(bank(b0 + 1), KmxT[64:128, p, mk * 128:(mk + 1) * 128], qmxb[64:128, p, :], True, True)

                    mx_s(0)
                    for n, (p, mk) in enumerate(its):
                        if n + 1 < len(its):
                            mx_s(n + 1)
                        b0 = 2 * (n % 2)
                        act(Em[b0], bank(b0), AF.Exp, scale=0.125)
                        act(Em[b0 + 1], bank(b0 + 1), AF.Exp, scale=0.125)
                        for q in range(4):
                            for e in range(2):
                                hd = 2 * p + e
                                mm(ACC[:, q, hd * 66:hd * 66 + 65], Em[b0 + e][:, q * 128:(q + 1) * 128],
                                   Vmx[:, mk, hd, 0:65], mk == 0 and hd == 0, mk == 1, skip=True)
                    accm = ACC[:, :, 0:264].rearrange("p q (h e) -> p q h e", e=66)
                    sc.op("dve", lambda e: e.reciprocal(rsm, accm[:, :, :, 64]), outs=[rsm], ins=[accm[:, :, :, 64]])
                    for q in range(4):
                        tt("dve", omxf[:, q * 256:(q + 1) * 256].rearrange("p (h e) -> p h e", e=64),
                           accm[:, q, :, 0:64], bc(rsm[:, q, :], [128, 4, 64]), ALU.mult)
                    tt("dve", ofm, omxf.rearrange("p (q n) -> p q n", n=256), gmxb, ALU.mult)
                    pt = bank_bf(2)
                    for i in range(2):
                        for q in range(4):
                            tr(pt[:, (i * 4 + q) * 128:(i * 4 + q + 1) * 128], ofm[:, q, i * 128:(i + 1) * 128])
                    cp("dve", omxT[:, :, blk * 512:(blk + 1) * 512], pt.rearrange("p (i t) -> p i t", t=512))

        steps = [(j, h, kt, hf) for j in range(NB) for h in range(4) for kt in range(NT) for hf in range(2)]
        SPH = 2 * NT

        def s_step(i):
            j, h, kt, hf = steps[i]
            ks = slice(kt * 128, (kt + 1) * 128)
            q0 = j * 512 + hf * 256
            r = i % 3
            mm(ps[:, r * 512:r * 512 + 256], KT[:, h, ks], QT[:, h, q0:q0 + 256], True, True)
            mm(ps[:, r * 512 + 256:r * 512 + 512], KT1[:, h, ks], QT[:, h, q0:q0 + 256], True, True)

        def exp_step(i):
            act(E[i % 4], bank(i % 3), AF.Exp, scale=0.125)

        def av_step(i):
            j, h, kt, hf = steps[i]
            mm(bank(4 + hf), VA[:, kt, h, 0:128], E[i % 4], kt == 0, kt == NT - 1)
            mm(bank(6 + hf), onesb, E[i % 4], kt == 0, kt == NT - 1)

        def epi1(j, h):
            for hf in range(2):
                cp("dve", tO[hf], bank(4 + hf))
                cp("act", rcp[hf][:, 0:256], ps[:, (6 + hf) * 512 + 256:(6 + hf) * 512 + 512])
                cp("act", rcp[hf][:, 256:512], ps[:, (6 + hf) * 512:(6 + hf) * 512 + 256])

        def epi2(j, h):
            for hf in range(2):
                tt("dve", tO[hf], tO[hf], rcp[hf], ALU.mult)
            for hf in range(2):
                hs = slice(hf * 256, (hf + 1) * 256)
                stt("dve", Of[:, hs], tO[hf][:, 256:512], nl[:, 0:1], tO[hf][:, 0:256], ALU.mult, ALU.add)
                tt("dve", pp[:, hs], rcp[hf][:, 0:256], rcp[hf][:, 256:512], ALU.mult)
            stt("dve", pp, pp, 128.0 * LN_EPS, pp, ALU.mult, ALU.mult)
            tt("dve", sqb, Of, Of, ALU.mult)

        def epi2m(j, h):
            mm(bank(3), onesb, sqb, True, True)
            tt("dve", rscr, bank(3), pp, ALU.add)
            dma_sp(rsc_d[0:1, :], rscr[0:1, :], "r0", xw_=[("rsc", 0)])
            dma_sp(sm_in, rsc_d[0].rearrange("(p c) -> p c", c=4), "r1", xr_=[("rsc", 0)])
            tt("pool", sm_out, sm_in, mhalf, ALU.pow)
            dma_sp(rsc_d[1].rearrange("(p c) -> p c", c=4), sm_out, "r2", xw_=[("rsc", 1)])
            dma_sp(pp, rsc_d[1].partition_broadcast(128), "r3", xr_=[("rsc", 1)])

        def epi2b(j, h):
            tt("dve", rscr, pp, gateT[:, h, j * 512:(j + 1) * 512], ALU.mult)
            stt("dve", mix[j % 2][:, h, :], Of, sgp1[:, 0:1], rscr, ALU.mult, ALU.mult)

        def c4_pre(j, q):
            t = 4 * j + q
            xs = xr[q]
            dma_sp(xs, x_d[s, t * 128:(t + 1) * 128, :], "xr%d" % q)
            stt("dve", xs, xs, stats[:, t, 0:1], gin, ALU.mult, ALU.mult)
            stt("dve", xs, gin, stats[:, t, 1:2], xs, ALU.mult, ALU.add)
            tt("dve", xs, xs, bin_, ALU.add)

        def c4_half(j, q, half):
            t = 4 * j + q
            xs = xr[q]
            mx = mix[j % 2]
            tsl = slice(t * 128, (t + 1) * 128)
            for c in range(8):
                if c < 4:
                    lhsT = mx[:, c, q * 128:(q + 1) * 128]
                elif c < 6:
                    lhsT = ocvT[:, c - 4, tsl]
                else:
                    lhsT = omxT[:, c - 6, tsl]
                mm(bank(3), lhsT, wg[slo[half]][:, c, :], c == 0, c == 7)
            hs = slice(half * 512, (half + 1) * 512)
            stt("dve", xs[:, hs], xs[:, hs], ALPHA, bank(3), ALU.mult, ALU.add)

        def c4_tail(j, q):
            t = 4 * j + q
            xs = xr[q]
            ln_stats(xs, st12b, mvb, st2[:, 0:1], st2[:, 1:2])
            stt("dve", xs, xs, st2[:, 0:1], g2, ALU.mult, ALU.mult)
            stt("dve", xs, g2, st2[:, 1:2], xs, ALU.mult, ALU.add)
            tt("dve", xs, xs, b2, ALU.add)
            dma_sp(y_d[s, t * 128:(t + 1) * 128, :], xs, "ys%d" % q)

        slo = [load_w(10), load_w(11)]
        for q in range(4):
            c4_pre(0, q)
        nsteps = len(steps)
        deferred = {}
        fifo = []
        pre_done = set((0, q) for q in range(4))

        def defer(at, fn):
            deferred.setdefault(at, []).append(fn)

        def mix_final(jb):
            for q in range(4):
                fifo.append((jb, q))

        def do_pre(jb, q):
            if jb < NB and (jb, q) not in pre_done:
                pre_done.add((jb, q))
                c4_pre(jb, q)

        cur = [None]
        s_step(0)
        s_step(1)
        for i in range(nsteps):
            j, h, kt, hf = steps[i]
            if i + 2 < nsteps:
                s_step(i + 2)
            exp_step(i)
            av_step(i)
            for fn in deferred.pop(i, []):
                fn()
            if kt == NT - 1 and hf == 1:
                epi1(j, h)
                defer(i + 5, lambda j=j, h=h: epi2(j, h))
                defer(i + 13, lambda j=j, h=h: epi2m(j, h))
                if h == 3:
                    defer(i + 34, lambda j=j, h=h: (epi2b(j, h), mix_final(j)))
                else:
                    defer(i + 34, lambda j=j, h=h: epi2b(j, h))
            if hf == 1 and kt == 7 and fifo:
                cur[0] = fifo.pop(0)
                c4_half(cur[0][0], cur[0][1], 0)
            elif hf == 1 and kt == 8 and cur[0] is not None:
                jb, q = cur[0]
                cur[0] = None
                c4_half(jb, q, 1)
                c4_tail(jb, q)
                defer(i + 20, lambda jb=jb, q=q: do_pre(jb + 1, q))
        for k in sorted(deferred):
            for fn in deferred[k]:
                fn()
        while fifo:
            jb, q = fifo.pop(0)
            do_pre(jb, q)
            c4_half(jb, q, 0)
            c4_half(jb, q, 1)
            c4_tail(jb, q)
            do_pre(jb + 1, q)

    sc.finalize()
    final_waits = [(k, v) for k, v in sc.dma_counts.items() if k.startswith("ys") or k.startswith("ws")]

    with contextlib.ExitStack() as es:
        sems = {}
        for k in semkeys:
            sems[k] = es.enter_context(nc.semaphore("e_" + k[1]))
        for name in sorted(dma_sem_names):
            sems[name] = es.enter_context(nc.semaphore("d_" + name))
        es.enter_context(nc.allow_low_precision("bf16 matmul operands, fp32 accumulation"))
        block = es.enter_context(nc.Block())

        @block.tensor
        def _(e):
            sc.emit("pe", e, sems)

        @block.scalar
        def _(e):
            sc.emit("act", e, sems)

        @block.vector
        def _(e):
            sc.emit("dve", e, sems)

        @block.gpsimd
        def _(e):
            sc.emit("pool", e, sems)

        @block.sync
        def _(e):
            sc.emit("sp", e, sems)
            for k, v in final_waits:
                e.wait_ge(sems[k], v)
    return nc


def _weight_groups(w_in, w_mem_kv, w_o):
    w = np.asarray(w_in, np.float32)[0]
    cols = []
    for h in range(4):
        base = np.arange(128)
        c, d = base // 64, base % 64
        dsw = np.where(d < 8, d + 8, np.where(d < 16, d - 8, d))
        q = h * 128 + base
        qp = h * 128 + c * 64 + dsw
        cols.append(np.concatenate([q, qp, 512 + q, 512 + qp]))
    for i in range(2):
        r = np.arange(128) + 128 * i
        cols.append(np.concatenate([2304 + r, 2560 + r, 2816 + r, 2048 + r]))
    cols.append(np.arange(1024, 1536))
    cols.append(np.arange(1536, 2048))
    cols.append(np.arange(3072, 3584))
    mats = [w[:, c] for c in cols]
    mats.append(np.asarray(w_mem_kv, np.float32)[0])
    wo = np.asarray(w_o, np.float32)[0]
    mats.append(wo[:, 0:512])
    mats.append(wo[:, 512:1024])
    out = np.empty((NG, 128, 8, 512), np.float32)
    for g, m in enumerate(mats):
        out[g] = m.reshape(8, 128, 512).transpose(1, 0, 2)
    return out.reshape(NG, 128, 4096)


def _perm_matrix():
    p = np.zeros((128, 128), np.float32)
    for r in range(128):
        d = r % 64
        k = r + 8 if d < 8 else (r - 8 if d < 16 else r)
        p[k, r] = 1.0
    return p


def _rope_tables():
    inv_freq = np.float64(ROPE_THETA) ** (-np.arange(0, 16, 2, dtype=np.float64) / 16.0)
    ang = np.arange(S, dtype=np.float64)[None, :] * inv_freq[:, None]
    cos, sin = np.cos(ang).astype(np.float32), np.sin(ang).astype(np.float32)
    ct = np.ones((128, S), np.float32)
    st = np.zeros((128, S), np.float32)
    for r in range(128):
        d = r % 64
        if d < 16:
            ct[r] = cos[d % 8]
            st[r] = -sin[d % 8] if d < 8 else sin[d % 8]
    return ct, st


_PROG = {}


def _run(x_all, mem_all, consts, ncores, nseq):
    if nseq not in _PROG:
        _PROG[nseq] = build_program(nseq)
    nc = _PROG[nseq]
    in_maps = []
    for c in range(ncores):
        m = dict(consts)
        m["x"] = np.ascontiguousarray(x_all[c * nseq:(c + 1) * nseq])
        m["mem"] = np.ascontiguousarray(mem_all[c * nseq:(c + 1) * nseq])
        in_maps.append(m)
    res = run_bass_kernel_spmd(nc, in_maps, core_ids=list(range(ncores)))
    return np.concatenate([np.asarray(r["y"]) for r in res.results], axis=0)


def _consts(in_ln_g, in_ln_b, w_in, w_mem_kv, lam_q1, lam_k1, lam_q2, lam_k2, subln_g, conv_w, w_o, ln_g, ln_b):
    ct, st = _rope_tables()
    cwh = np.asarray(conv_w, np.float32)[0].T.reshape(2, 128, 3).transpose(1, 0, 2).reshape(128, 6)
    return {
        "wsrc": _weight_groups(w_in, w_mem_kv, w_o),
        "lnp": np.stack([np.asarray(a, np.float32).reshape(D) for a in (in_ln_g, in_ln_b, ln_g, ln_b)]),
        "ctab": ct, "stab": st,
        "ident": np.eye(128, dtype=np.float32),
        "permm": _perm_matrix(),
        "cw": np.ascontiguousarray(cwh),
        "subg": np.asarray(subln_g, np.float32).reshape(128),
        "lnfm": np.ascontiguousarray(np.concatenate([np.asarray(a, np.float32).reshape(8, 128).T for a in (in_ln_g, in_ln_b)], axis=1)),
        "lamv": np.concatenate([np.asarray(a, np.float32).reshape(64) for a in (lam_q1, lam_k1, lam_q2, lam_k2)]),
    }


def kernel(x_prompt, x_sample, mem_prompt, mem_sample, in_ln_g, in_ln_b, w_in, w_mem_kv,
           lam_q1, lam_k1, lam_q2, lam_k2, subln_g, conv_w, w_o, ln_g, ln_b):
    x_prompt = np.asarray(x_prompt, np.float32)
    x_sample = np.asarray(x_sample, np.float32)
    nb_p = x_prompt.shape[0]
    x_all = np.concatenate([x_prompt, x_sample], axis=0)
    mem_all = np.concatenate([np.asarray(mem_prompt, np.float32), np.asarray(mem_sample, np.float32)], axis=0)
    consts = _consts(in_ln_g, in_ln_b, w_in, w_mem_kv, lam_q1, lam_k1, lam_q2, lam_k2,
                     subln_g, conv_w, w_o, ln_g, ln_b)
    y = _run(x_all, mem_all, consts, NCORES, SEQ_PER_CORE)
    return (np.ascontiguousarray(y[:nb_p]), np.ascontiguousarray(y[nb_p:]))
```

```python
import contextlib
import math
import numpy as np
import concourse.bass as bass
import concourse.mybir as mybir
from concourse.bass_utils import run_bass_kernel_spmd

F32 = mybir.dt.float32
BF16 = mybir.dt.bfloat16
AF = mybir.ActivationFunctionType
ALU = mybir.AluOpType
AX = mybir.AxisListType

D = 1024
S = 2048
NT = 16
NB = 4
NMEM = 256
NCORES = 8
SEQ_PER_CORE = 6
NG = 12
LN_EPS = 1e-5
ALPHA = 2.0 ** 0.25
LAM_INIT = 0.2
ROPE_THETA = 500000.0
CELL = 64


class Op:
    __slots__ = ("eng", "fn", "deps", "idx", "sig", "cnt", "sem", "dma")

    def __init__(self, eng, fn, sem):
        self.eng = eng
        self.fn = fn
        self.deps = set()
        self.sig = False
        self.cnt = 0
        self.sem = sem
        self.dma = sem is not None


class Sched:
    def __init__(self):
        self.engs = {"pe": [], "act": [], "dve": [], "pool": [], "sp": []}
        self.cells = {}
        self.dma_counts = {}

    @staticmethod
    def _cells_of(ap):
        name = ap.tensor.name
        if name not in ("arena", "ps"):
            return ()
        esz = 2 if ap.dtype == BF16 else 4
        pairs = ap.ap
        row = pairs[0][0]
        start = int(ap.offset) % row if row > 0 else int(ap.offset)
        ext = 1
        for st, cn in pairs[1:]:
            ext += (cn - 1) * st
        b0 = start * esz
        b1 = (start + ext) * esz
        c0 = b0 // (CELL * 4)
        c1 = (b1 - 1) // (CELL * 4)
        return [(name, c) for c in range(c0, c1 + 1)]

    def _dep(self, o, p, kind):
        if p is o:
            return
        if p.dma:
            o.deps.add(p)
            return
        if p.eng == o.eng:
            if o.dma:
                o.deps.add(p)
                return
            if o.eng == "pe":
                return
            o.deps.add(p)
            return
        o.deps.add(p)

    def op(self, eng, fn, outs=(), ins=(), xr=(), xw=(), sem=None):
        o = Op(eng, fn, sem)
        o.idx = len(self.engs[eng])
        self.engs[eng].append(o)
        rc = list(xr)
        wc = list(xw)
        for a in ins:
            rc.extend(self._cells_of(a))
        for a in outs:
            wc.extend(self._cells_of(a))
        cells = self.cells
        for r in rc:
            c = cells.get(r)
            if c is not None and c[0] is not None:
                self._dep(o, c[0], "raw")
        for r in wc:
            c = cells.get(r)
            if c is not None:
                if c[0] is not None:
                    self._dep(o, c[0], "waw")
                for rd in c[1].values():
                    self._dep(o, rd, "war")
        key = ("dma", id(o)) if o.dma else eng
        for r in rc:
            c = cells.get(r)
            if c is None:
                c = cells[r] = [None, {}]
            c[1][key] = o
        for r in wc:
            cells[r] = [o, {}]
        if o.dma:
            n = self.dma_counts.get(sem, 0) + 16
            self.dma_counts[sem] = n
            o.cnt = n
        return o

    def finalize(self):
        for ops in self.engs.values():
            for o in ops:
                for p in o.deps:
                    if not p.dma:
                        p.sig = True
        for ops in self.engs.values():
            n = 0
            for o in ops:
                if not o.dma and o.sig:
                    n += 1
                    o.cnt = n

    def emit(self, eng_name, eng, sems):
        waited = {}
        for o in self.engs[eng_name]:
            need = {}
            for p in o.deps:
                key = p.sem if p.dma else ("eng", p.eng)
                if waited.get(key, 0) >= p.cnt:
                    continue
                if need.get(key, 0) < p.cnt:
                    need[key] = p.cnt
            for key, cnt in need.items():
                eng.wait_ge(sems[key], cnt)
                waited[key] = cnt
            inst = o.fn(eng)
            if o.dma:
                inst.then_inc(sems[o.sem], 16)
            elif o.sig:
                inst.then_inc(sems[("eng", o.eng)], 1)


def build_program(nseq):
    nc = bass.Bass("TRN2", target_bir_lowering=False)
    x_d = nc.dram_tensor("x", [nseq, S, D], F32, kind="ExternalInput").ap()
    mem_d = nc.dram_tensor("mem", [nseq, NMEM, D], F32, kind="ExternalInput").ap()
    wsrc_d = nc.dram_tensor("wsrc", [NG, 128, 4096], F32, kind="ExternalInput").ap()
    lnp_d = nc.dram_tensor("lnp", [4, D], F32, kind="ExternalInput").ap()
    ctab_d = nc.dram_tensor("ctab", [128, S], F32, kind="ExternalInput").ap()
    stab_d = nc.dram_tensor("stab", [128, S], F32, kind="ExternalInput").ap()
    ident_d = nc.dram_tensor("ident", [128, 128], F32, kind="ExternalInput").ap()
    permm_d = nc.dram_tensor("permm", [128, 128], F32, kind="ExternalInput").ap()
    cw_d = nc.dram_tensor("cw", [128, 6], F32, kind="ExternalInput").ap()
    sg_d = nc.dram_tensor("subg", [128], F32, kind="ExternalInput").ap()
    lam_d = nc.dram_tensor("lamv", [256], F32, kind="ExternalInput").ap()
    lnfm_d = nc.dram_tensor("lnfm", [128, 16], F32, kind="ExternalInput").ap()
    y_d = nc.dram_tensor("y", [nseq, S, D], F32, kind="ExternalOutput").ap()
    wsc_d = nc.dram_tensor("wsc", [NG, 128, 4096], BF16, kind="Internal").ap()
    rsc_d = nc.dram_tensor("rsc", [2, 512], F32, kind="Internal").ap()

    off = [0]

    def alloc(words):
        words = (words + CELL - 1) // CELL * CELL
        o = off[0]
        off[0] += words
        return o

    o_gin, o_bin, o_g2, o_b2 = alloc(1024), alloc(1024), alloc(1024), alloc(1024)
    o_ctab, o_stab = alloc(2048), alloc(2048)
    o_idb = alloc(64)
    o_ones = alloc(64)
    o_permb = alloc(64)
    o_sgp1 = alloc(1)
    o_lnfm = alloc(16)
    o_sgp = alloc(128)
    o_cw = alloc(8)
    o_lamv = alloc(256)
    o_lprod = alloc(128)
    o_ls = alloc(2)
    o_le = alloc(2)
    o_nl = alloc(1)
    o_stats = alloc(32)
    o_st12 = alloc(12)
    o_mv = alloc(2)
    o_st12b = alloc(12)
    o_mvb = alloc(2)
    o_st2 = alloc(2)
    o_rs = alloc(8)
    o_rsl = alloc(4)
    o_ss = alloc(4)
    o_rinv = alloc(4)
    o_rsm = alloc(16)
    o_mhalf = alloc(4)
    o_smin = alloc(4)
    o_smout = alloc(4)
    o_epsc = alloc(4)
    o_eps128 = alloc(4)
    o_tmpv = alloc(4)
    o_wg = [alloc(2048), alloc(2048)]
    o_QT = alloc(4096)
    o_KT = alloc(4096)
    o_KT1 = alloc(4096)
    o_VA = alloc(4160)
    o_gate = alloc(4096)
    o_ocv = alloc(2048)
    o_omx = alloc(2048)
    o_kmx = alloc(256)
    o_vmx = alloc(264)
    shared0 = off[0]
    o_hT = alloc(8192)
    tmp0 = off[0]
    o_xt = [alloc(1024), alloc(1024), alloc(1024)]
    o_hb = [alloc(512), alloc(512)]
    endA = off[0]
    off[0] = tmp0
    o_memb = alloc(1024)
    o_memT = alloc(1024)
    endM = off[0]
    off[0] = tmp0
    o_tA, o_tB = alloc(512), alloc(512)
    o_qtmp = [alloc(256) for _ in range(4)]
    endR = off[0]
    off[0] = tmp0
    o_csb, o_sgc, o_bgb, o_u, o_yc = alloc(512), alloc(512), alloc(512), alloc(2052), alloc(512)
    endC = off[0]
    off[0] = tmp0
    o_gtmp = alloc(512)
    off[0] = tmp0
    o_qmxb = alloc(512)
    o_gmxb = alloc(1024)
    o_Em = [alloc(256) for _ in range(4)]
    o_omxf = alloc(1024)
    o_ofm = alloc(512)
    endG8 = off[0]
    off[0] = shared0
    o_E = [alloc(256) for _ in range(4)]
    o_mix = [alloc(1024), alloc(1024)]
    o_rcp = [alloc(512), alloc(512)]
    o_tO = [alloc(512), alloc(512)]
    o_Of = alloc(512)
    o_rscr = alloc(512)
    o_pp = alloc(512)
    o_sqb = alloc(256)
    o_xr = [alloc(1024) for _ in range(4)]
    endP = off[0]
    AW = max(endA, endM, endR, endC, endG8, endP)

    arena = nc.alloc_sbuf_tensor("arena", [128, AW], F32)
    ps = nc.alloc_psum_tensor("ps", [128, 4096], F32)

    def f32v(o, n):
        return arena[:, o:o + n]

    def bfv(o, nwords):
        return arena[:, o:o + nwords].bitcast(BF16)

    def bank(b, n=512):
        return ps[:, b * 512:b * 512 + n]

    def bank_bf(b):
        return ps[:, b * 512:(b + 1) * 512].bitcast(BF16)

    gin, bin_, g2, b2 = f32v(o_gin, 1024), f32v(o_bin, 1024), f32v(o_g2, 1024), f32v(o_b2, 1024)
    ctab, stab = f32v(o_ctab, 2048), f32v(o_stab, 2048)
    idb = bfv(o_idb, 64)
    onesb = bfv(o_ones, 64)
    permb = bfv(o_permb, 64)
    sgp1 = f32v(o_sgp1, 1)
    lnfm = f32v(o_lnfm, 16)
    sgp = f32v(o_sgp, 128)
    cw = f32v(o_cw, 6)
    lamv = f32v(o_lamv, 256)
    lprod = f32v(o_lprod, 128)
    ls, le, nl = f32v(o_ls, 2), f32v(o_le, 2), f32v(o_nl, 1)
    stats = f32v(o_stats, 32).rearrange("p (t k) -> p t k", k=2)
    st12, mv = f32v(o_st12, 12), f32v(o_mv, 2)
    st12b, mvb, st2 = f32v(o_st12b, 12), f32v(o_mvb, 2), f32v(o_st2, 2)
    rs = f32v(o_rs, 8).rearrange("p (q c) -> p q c", c=2)
    rsl, ss, rinv = f32v(o_rsl, 4), f32v(o_ss, 4), f32v(o_rinv, 4)
    rsm = f32v(o_rsm, 16).rearrange("p (q h) -> p q h", h=4)
    sm_in, sm_out = f32v(o_smin, 4), f32v(o_smout, 4)
    mhalf, epsc, eps128, tmpv = f32v(o_mhalf, 4), f32v(o_epsc, 4), f32v(o_eps128, 4), f32v(o_tmpv, 4)
    wg = [bfv(o, 2048).rearrange("p (c n) -> p c n", n=512) for o in o_wg]
    wg_flat = [bfv(o, 2048) for o in o_wg]
    QT = bfv(o_QT, 4096).rearrange("p (h t) -> p h t", t=S)
    KT = bfv(o_KT, 4096).rearrange("p (h t) -> p h t", t=S)
    KT1 = bfv(o_KT1, 4096).rearrange("p (h t) -> p h t", t=S)
    VA = bfv(o_VA, 4160).rearrange("p (t h e) -> p t h e", h=4, e=130)
    gateT = bfv(o_gate, 4096).rearrange("p (h t) -> p h t", t=S)
    ocvT = bfv(o_ocv, 2048).rearrange("p (i t) -> p i t", t=S)
    omxT = bfv(o_omx, 2048).rearrange("p (i t) -> p i t", t=S)
    KmxT = bfv(o_kmx, 256).rearrange("p (i t) -> p i t", t=256)
    Vmx = bfv(o_vmx, 264).rearrange("p (t h e) -> p t h e", h=4, e=66)
    hT = bfv(o_hT, 8192).rearrange("p (b c t) -> p b c t", c=8, t=512)
    xt = [f32v(o, 1024) for o in o_xt]
    hb = [bfv(o, 512) for o in o_hb]
    memb = bfv(o_memb, 1024).rearrange("p (t d) -> p t d", d=1024)
    memT = bfv(o_memT, 1024).rearrange("p (c t) -> p c t", t=256)
    tA, tB = f32v(o_tA, 512), f32v(o_tB, 512)
    qtmp = [bfv(o, 256) for o in o_qtmp]
    csb, sgc, bgb, yc = f32v(o_csb, 512), f32v(o_sgc, 512), f32v(o_bgb, 512), f32v(o_yc, 512)
    u = f32v(o_u, 2050)
    gtmp = f32v(o_gtmp, 512)
    qmxb = bfv(o_qmxb, 512).rearrange("p (i t) -> p i t", t=512)
    gmxb = f32v(o_gmxb, 1024).rearrange("p (q n) -> p q n", n=256)
    Em = [bfv(o, 256) for o in o_Em]
    omxf = f32v(o_omxf, 1024)
    ofm = bfv(o_ofm, 512).rearrange("p (q n) -> p q n", n=256)
    E = [bfv(o, 256) for o in o_E]
    mix = [bfv(o, 1024).rearrange("p (h t) -> p h t", t=512) for o in o_mix]
    rcp = [f32v(o, 512) for o in o_rcp]
    tO = [f32v(o, 512) for o in o_tO]
    Of = f32v(o_Of, 512)
    rscr = f32v(o_rscr, 512)
    pp = f32v(o_pp, 512)
    sqb = bfv(o_sqb, 256)
    xr = [f32v(o, 1024) for o in o_xr]

    sc = Sched()
    semkeys = [("eng", e) for e in ("pe", "act", "dve", "pool")]
    dma_sem_names = set()

    def dsem(name):
        dma_sem_names.add(name)
        return name

    def dma_sp(out, in_, sem, xr_=(), xw_=()):
        sc.op("sp", lambda e: e.dma_start(out=out, in_=in_), outs=[out], ins=[in_],
              xr=xr_, xw=xw_, sem=dsem(sem))

    def dma_pool(out, in_, sem, xr_=(), xw_=()):
        sc.op("pool", lambda e: e.dma_start(out=out, in_=in_), outs=[out], ins=[in_],
              xr=xr_, xw=xw_, sem=dsem(sem))

    def mm(out, lhsT, rhs, start, stop, skip=False):
        sc.op("pe", lambda e: e.matmul(out, lhsT, rhs, start=start, stop=stop, skip_group_check=skip),
              outs=[out], ins=[lhsT, rhs])

    def tr(out, in_):
        sc.op("pe", lambda e: e.transpose(out, in_, idb), outs=[out], ins=[in_, idb])

    def act(out, in_, func, bias=0.0, scale=1.0, extra=()):
        ins = [in_] + [a for a in (bias, scale) if not isinstance(a, float)] + list(extra)
        sc.op("act", lambda e: e.activation(out, in_, func, bias=bias, scale=scale),
              outs=[out], ins=ins)

    def tt(eng, out, in0, in1, op, extra=()):
        sc.op(eng, lambda e: e.tensor_tensor(out, in0, in1, op), outs=[out], ins=[in0, in1] + list(extra))

    def ts(eng, out, in0, s1, s2, op0, op1=None):
        ins = [in0] + [a for a in (s1, s2) if a is not None and not isinstance(a, float)]
        if op1 is None:
            sc.op(eng, lambda e: e.tensor_scalar(out, in0, s1, s2, op0), outs=[out], ins=ins)
        else:
            sc.op(eng, lambda e: e.tensor_scalar(out, in0, s1, s2, op0, op1), outs=[out], ins=ins)

    def stt(eng, out, in0, scalar, in1, op0, op1):
        ins = [in0, in1] + ([scalar] if not isinstance(scalar, float) else [])
        sc.op(eng, lambda e: e.scalar_tensor_tensor(out, in0, scalar, in1, op0, op1),
              outs=[out], ins=ins)

    def cp(eng, out, in_):
        if eng == "act":
            act(out, in_, AF.Copy)
        else:
            sc.op(eng, lambda e: e.tensor_copy(out, in_), outs=[out], ins=[in_])

    def memset(eng, out, val):
        sc.op(eng, lambda e: e.memset(out, val), outs=[out])

    def bc(ap2d, shape):
        return ap2d.unsqueeze(len(ap2d.shape)).broadcast_to(shape)

    for i, dst in enumerate((gin, bin_, g2, b2)):
        dma_sp(dst, lnp_d[i].partition_broadcast(128), "c%d" % i)
    dma_sp(ctab, ctab_d, "c4")
    dma_sp(stab, stab_d, "c5")
    dma_sp(cw, cw_d, "c6")
    dma_sp(sgp, sg_d.partition_broadcast(128), "c7")
    dma_sp(sgp1, sg_d.rearrange("(p o) -> p o", o=1), "c10")
    dma_sp(lnfm, lnfm_d, "c11")
    dma_sp(lamv, lam_d.partition_broadcast(128), "c8")
    dma_pool(idb, ident_d, "c9")
    dma_pool(permb, permm_d, "c12")
    ts("dve", sgp1, sgp1, (1.0 - LAM_INIT) * math.sqrt(128.0), None, ALU.mult)
    memset("pool", onesb, 1.0)
    lv = lamv.rearrange("p (a b d) -> p a b d", a=2, b=2)
    tt("dve", lprod.rearrange("p (a d) -> p a d", d=64), lv[:, :, 0, :], lv[:, :, 1, :], ALU.mult)
    sc.op("dve", lambda e: e.tensor_reduce(ls, lprod.rearrange("p (a d) -> p a d", d=64), AX.X, ALU.add),
          outs=[ls], ins=[lprod])
    act(le, ls, AF.Exp)
    stt("dve", nl, le[:, 0:1], -1.0, le[:, 1:2], ALU.mult, ALU.add)
    ts("dve", nl, nl, -LAM_INIT, None, ALU.add)
    memset("pool", mhalf, -0.5)
    memset("pool", epsc, LN_EPS)
    memset("pool", eps128, 128.0 * LN_EPS)
    memset("pool", KT[64:128, :, :], 0.0)
    memset("pool", KT1[0:64, :, :], 0.0)
    memset("pool", VA[:, :, :, 128:130], 1.0)
    memset("pool", Vmx[:, :, :, 64:66], 1.0)
    for g in range(NG):
        sl = g % 2
        dma_pool(wg_flat[sl], wsrc_d[g], "wc%d" % sl)
        dma_sp(wsc_d[g], wg_flat[sl], "ws%d" % sl, xw_=[("wsc", g)])

    wslot = [0]

    def load_w(g):
        sl = wslot[0] % 2
        wslot[0] += 1
        dma_sp(wg_flat[sl], wsc_d[g], "wl%d" % sl, xr_=[("wsc", g)])
        return sl

    psrot = [0]

    def next_pair():
        p = psrot[0] % 2
        psrot[0] += 1
        return 2 * p, 2 * p + 1

    def fm_tile(bk, sl, i, blk):
        for c in range(8):
            mm(bank(bk), wg[sl][:, c, i * 128:(i + 1) * 128], hT[:, blk, c, :], c == 0, c == 7)

    def ln_stats(src, st_, mv_, rstd_out, nmr_out):
        sc.op("dve", lambda e: e.bn_stats(st_[:, 0:6], src[:, 0:512]), outs=[st_[:, 0:6]], ins=[src[:, 0:512]])
        sc.op("dve", lambda e: e.bn_stats(st_[:, 6:12], src[:, 512:1024]), outs=[st_[:, 6:12]], ins=[src[:, 512:1024]])
        sc.op("dve", lambda e: e.bn_aggr(mv_, st_), outs=[mv_], ins=[st_])
        tt("pool", tmpv[:, 0:1], mv_[:, 1:2], epsc[:, 0:1], ALU.add)
        tt("pool", rstd_out, tmpv[:, 0:1], mhalf[:, 0:1], ALU.pow)
        stt("dve", nmr_out, mv_[:, 0:1], -1.0, rstd_out, ALU.mult, ALU.mult)

    ystores = {}
    ACC = ps[:, 4 * 512:8 * 512].rearrange("p (q n) -> p q n", n=512)

    for s in range(nseq):
        slkv = load_w(9)
        dma_pool(memb, mem_d[s].rearrange("(t p) d -> p t d", p=128), "mem")
        for t in range(2):
            pt = bank_bf(7)
            for c in range(8):
                tr(pt[:, c * 128:(c + 1) * 128], memb[:, t, c * 128:(c + 1) * 128])
            cp("act", memT[:, :, t * 128:(t + 1) * 128], pt.rearrange("p (c t) -> p c t", t=128))
        for i in range(2):
            bk = next_pair()[0]
            for c in range(8):
                mm(bank(bk, 256), wg[slkv][:, c, i * 128:(i + 1) * 128], memT[:, c, :], c == 0, c == 7)
            cp("act", KmxT[:, i, :], bank(bk, 256))
        for t in range(2):
            bk = next_pair()[0]
            for c in range(8):
                mm(bank(bk, 256), memT[:, c, t * 128:(t + 1) * 128], wg[slkv][:, c, 256:512], c == 0, c == 7)
            cp("dve", Vmx[:, t, :, 0:64], bank(bk, 256).rearrange("p (h e) -> p h e", e=64))

        sl_next = load_w(0)
        for t in range(NT):
            xs = xt[t % 3]
            hbt = hb[t % 2]
            dma_sp(xs, x_d[s, t * 128:(t + 1) * 128, :], "x%d" % (t % 3))
            ln_stats(xs, st12, mv, stats[:, t, 0:1], stats[:, t, 1:2])
            act(hbt, xs, AF.Identity, bias=stats[:, t, 1:2], scale=stats[:, t, 0:1])
            pt = bank_bf(6 + (t % 2))
            for c in range(8):
                tr(pt[:, c * 128:(c + 1) * 128], hbt[:, c * 128:(c + 1) * 128])
            for c in range(8):
                act(hT[:, t // 4, c, (t % 4) * 128:(t % 4 + 1) * 128], pt[:, c * 128:(c + 1) * 128],
                    AF.Identity, bias=lnfm[:, 8 + c:9 + c], scale=lnfm[:, c:c + 1], extra=[pt])

        for g in range(9):
            sl = sl_next
            if g < 8:
                sl_next = load_w(g + 1)
            if g < 4:
                h = g
                for blk in range(NB + 1):
                    if blk < NB:
                        sb = 4 * (blk % 2)
                        fm_tile(sb, sl, 0, blk)
                        fm_tile(sb + 2, sl, 2, blk)
                        cp("act", qtmp[2 * (blk % 2)], bank(sb))
                        cp("act", qtmp[2 * (blk % 2) + 1], bank(sb + 2))
                    if blk >= 1:
                        pb_ = blk - 1
                        sb = 4 * (pb_ % 2)
                        cs = slice(pb_ * 512, (pb_ + 1) * 512)
                        mm(bank(sb + 1), permb, qtmp[2 * (pb_ % 2)], True, True)
                        mm(bank(sb + 3), permb, qtmp[2 * (pb_ % 2) + 1], True, True)
                        tt("dve", tA, bank(sb), ctab[:, cs], ALU.mult, extra=[qtmp[2 * (pb_ % 2)]])
                        tt("dve", tB, bank(sb + 1), stab[:, cs], ALU.mult)
                        tt("dve", QT[:, h, cs], tA, tB, ALU.add)
                        tt("dve", tA, bank(sb + 2), ctab[:, cs], ALU.mult, extra=[qtmp[2 * (pb_ % 2) + 1]])
                        tt("dve", tB, bank(sb + 3), stab[:, cs], ALU.mult)
                        tt("dve", KT[0:64, h, cs], tA[0:64, :], tB[0:64, :], ALU.add)
                        tt("dve", KT1[64:128, h, cs], tA[64:128, :], tB[64:128, :], ALU.add)
            elif g < 6:
                i = g - 4
                memset("pool", u[:, 0:1], 0.0)
                memset("pool", u[:, 2049:2050], 0.0)
                for blk in range(NB):
                    ba, bb = next_pair()
                    fm_tile(ba, sl, 0, blk)
                    cp("act", csb, bank(ba))
                    fm_tile(bb, sl, 1, blk)
                    tt("dve", u[:, 1 + blk * 512:1 + (blk + 1) * 512], bank(bb), csb, ALU.mult)
                for blk in range(NB):
                    ba, bb = next_pair()
                    fm_tile(ba, sl, 2, blk)
                    act(sgc, bank(ba), AF.Silu)
                    fm_tile(bb, sl, 3, blk)
                    tt("dve", bgb, bank(bb), sgc, ALU.mult)
                    b0 = blk * 512
                    ts("dve", yc, u[:, b0:b0 + 512], cw[:, 3 * i:3 * i + 1], None, ALU.mult)
                    stt("dve", yc, u[:, b0 + 1:b0 + 513], cw[:, 3 * i + 1:3 * i + 2], yc, ALU.mult, ALU.add)
                    stt("dve", yc, u[:, b0 + 2:b0 + 514], cw[:, 3 * i + 2:3 * i + 3], yc, ALU.mult, ALU.add)
                    tt("dve", ocvT[:, i, b0:b0 + 512], yc, bgb, ALU.mult)
            elif g == 6:
                for t in range(NT):
                    bk = next_pair()[t % 2]
                    for c in range(8):
                        mm(bank(bk), hT[:, t // 4, c, (t % 4) * 128:(t % 4 + 1) * 128], wg[sl][:, c, :], c == 0, c == 7)
                    cp("act", VA[:, t, :, 0:128], bank(bk).rearrange("p (h e) -> p h e", e=128))
            elif g == 7:
                for i in range(4):
                    for blk in range(NB):
                        bk = next_pair()[blk % 2]
                        fm_tile(bk, sl, i, blk)
                        act(gateT[:, i, blk * 512:(blk + 1) * 512], bank(bk), AF.Silu)
            else:
                for blk in range(NB):
                    for i in range(2):
                        bk = next_pair()[i]
                        fm_tile(bk, sl, i, blk)
                        cp("dve", qmxb[:, i, :], bank(bk))
                    for q in range(4):
                        bk = next_pair()[q % 2]
                        for c in range(8):
                            mm(bank(bk, 256), hT[:, blk, c, q * 128:(q + 1) * 128], wg[sl][:, c, 256:512], c == 0, c == 7)
                        act(gmxb[:, q, :], bank(bk, 256), AF.Silu)
                    its = [(p, mk) for p in range(2) for mk in range(2)]

                    def mx_s(n):
                        p, mk = its[n]
                        b0 = 2 * (n % 2)
                        mm(bank(b0), KmxT[0:64, p, mk * 128:(mk + 1) * 128], qmxb[0:64, p, :], True, True)
                        mm(bank(b0 + 1), KmxT[64:128, p, mk * 128:(mk + 1) * 128], qmxb[64:128, p, :], True, True)

                    mx_s(0)
                    for n, (p, mk) in enumerate(its):
                        if n + 1 < len(its):
                            mx_s(n + 1)
                        b0 = 2 * (n % 2)
                        act(Em[b0], bank(b0), AF.Exp, scale=0.125)
                        act(Em[b0 + 1], bank(b0 + 1), AF.Exp, scale=0.125)
                        for q in range(4):
                            for e in range(2):
                                hd = 2 * p + e
                                mm(ACC[:, q, hd * 66:hd * 66 + 65], Em[b0 + e][:, q * 128:(q + 1) * 128],
                                   Vmx[:, mk, hd, 0:65], mk == 0 and hd == 0, mk == 1, skip=True)
                    accm = ACC[:, :, 0:264].rearrange("p q (h e) -> p q h e", e=66)
                    sc.op("dve", lambda e: e.reciprocal(rsm, accm[:, :, :, 64]), outs=[rsm], ins=[accm[:, :, :, 64]])
                    for q in range(4):
                        tt("dve", omxf[:, q * 256:(q + 1) * 256].rearrange("p (h e) -> p h e", e=64),
                           accm[:, q, :, 0:64], bc(rsm[:, q, :], [128, 4, 64]), ALU.mult)
                    tt("dve", ofm, omxf.rearrange("p (q n) -> p q n", n=256), gmxb, ALU.mult)
                    pt = bank_bf(2)
                    for i in range(2):
                        for q in range(4):
                            tr(pt[:, (i * 4 + q) * 128:(i * 4 + q + 1) * 128], ofm[:, q, i * 128:(i + 1) * 128])
                    cp("dve", omxT[:, :, blk * 512:(blk + 1) * 512], pt.rearrange("p (i t) -> p i t", t=512))

        steps = [(j, h, kt, hf) for j in range(NB) for h in range(4) for kt in range(NT) for hf in range(2)]
        SPH = 2 * NT

        def s_step(i):
            j, h, kt, hf = steps[i]
            ks = slice(kt * 128, (kt + 1) * 128)
            q0 = j * 512 + hf * 256
            r = i % 3
            mm(ps[:, r * 512:r * 512 + 256], KT[:, h, ks], QT[:, h, q0:q0 + 256], True, True)
            mm(ps[:, r * 512 + 256:r * 512 + 512], KT1[:, h, ks], QT[:, h, q0:q0 + 256], True, True)

        def exp_step(i):
            act(E[i % 4], bank(i % 3), AF.Exp, scale=0.125)

        def av_step(i):
            j, h, kt, hf = steps[i]
            mm(bank(4 + hf), VA[:, kt, h, 0:128], E[i % 4], kt == 0, kt == NT - 1)
            mm(bank(6 + hf), onesb, E[i % 4], kt == 0, kt == NT - 1)

        def epi1(j, h):
            for hf in range(2):
                cp("dve", tO[hf], bank(4 + hf))
                cp("act", rcp[hf][:, 0:256], ps[:, (6 + hf) * 512 + 256:(6 + hf) * 512 + 512])
                cp("act", rcp[hf][:, 256:512], ps[:, (6 + hf) * 512:(6 + hf) * 512 + 256])

        def epi2(j, h):
            for hf in range(2):
                tt("dve", tO[hf], tO[hf], rcp[hf], ALU.mult)
            for hf in range(2):
                hs = slice(hf * 256, (hf + 1) * 256)
                stt("dve", Of[:, hs], tO[hf][:, 256:512], nl[:, 0:1], tO[hf][:, 0:256], ALU.mult, ALU.add)
                tt("dve", pp[:, hs], rcp[hf][:, 0:256], rcp[hf][:, 256:512], ALU.mult)
            stt("dve", pp, pp, 128.0 * LN_EPS, pp, ALU.mult, ALU.mult)
            tt("dve", sqb, Of, Of, ALU.mult)

        def epi2m(j, h):
            mm(bank(3), onesb, sqb, True, True)
            tt("dve", rscr, bank(3), pp, ALU.add)
            dma_sp(rsc_d[0:1, :], rscr[0:1, :], "r0", xw_=[("rsc", 0)])
            dma_sp(sm_in, rsc_d[0].rearrange("(p c) -> p c", c=4), "r1", xr_=[("rsc", 0)])
            tt("pool", sm_out, sm_in, mhalf, ALU.pow)
            dma_sp(rsc_d[1].rearrange("(p c) -> p c", c=4), sm_out, "r2", xw_=[("rsc", 1)])
            dma_sp(pp, rsc_d[1].partition_broadcast(128), "r3", xr_=[("rsc", 1)])

        def epi2b(j, h):
            tt("dve", rscr, pp, gateT[:, h, j * 512:(j + 1) * 512], ALU.mult)
            stt("dve", mix[j % 2][:, h, :], Of, sgp1[:, 0:1], rscr, ALU.mult, ALU.mult)

        def c4_pre(j, q):
            t = 4 * j + q
            xs = xr[q]
            dma_sp(xs, x_d[s, t * 128:(t + 1) * 128, :], "xr%d" % q)
            stt("dve", xs, xs, stats[:, t, 0:1], gin, ALU.mult, ALU.mult)
            stt("dve", xs, gin, stats[:, t, 1:2], xs, ALU.mult, ALU.add)
            tt("dve", xs, xs, bin_, ALU.add)

        def c4_half(j, q, half, bk=3):
            t = 4 * j + q
            xs = xr[q]
            mx = mix[j % 2]
            tsl = slice(t * 128, (t + 1) * 128)
            for c in range(8):
                if c < 4:
                    lhsT = mx[:, c, q * 128:(q + 1) * 128]
                elif c < 6:
                    lhsT = ocvT[:, c - 4, tsl]
                else:
                    lhsT = omxT[:, c - 6, tsl]
                mm(bank(bk), lhsT, wg[slo[half]][:, c, :], c == 0, c == 7)
            hs = slice(half * 512, (half + 1) * 512)
            stt("dve", xs[:, hs], xs[:, hs], ALPHA, bank(bk), ALU.mult, ALU.add)

        def c4_tail(j, q):
            t = 4 * j + q
            xs = xr[q]
            ln_stats(xs, st12b, mvb, st2[:, 0:1], st2[:, 1:2])
            stt("dve", xs, xs, st2[:, 0:1], g2, ALU.mult, ALU.mult)
            stt("dve", xs, g2, st2[:, 1:2], xs, ALU.mult, ALU.add)
            tt("dve", xs, xs, b2, ALU.add)
            dma_sp(y_d[s, t * 128:(t + 1) * 128, :], xs, "ys%d" % q)

        slo = [load_w(10), load_w(11)]
        for q in range(4):
            c4_pre(0, q)
        nsteps = len(steps)
        deferred = {}
        fifo = []
        pre_done = set((0, q) for q in range(4))

        def defer(at, fn):
            deferred.setdefault(at, []).append(fn)

        def mix_final(jb):
            for q in range(4):
                fifo.append((jb, q))

        def do_pre(jb, q):
            if jb < NB and (jb, q) not in pre_done:
                pre_done.add((jb, q))
                c4_pre(jb, q)

        cur = [None]
        s_step(0)
        s_step(1)
        for i in range(nsteps):
            j, h, kt, hf = steps[i]
            if i + 2 < nsteps:
                s_step(i + 2)
            exp_step(i)
            av_step(i)
            for fn in deferred.pop(i, []):
                fn()
            if kt == NT - 1 and hf == 1:
                epi1(j, h)
                defer(i + 5, lambda j=j, h=h: epi2(j, h))
                defer(i + 13, lambda j=j, h=h: epi2m(j, h))
                if h == 3:
                    defer(i + 34, lambda j=j, h=h: (epi2b(j, h), mix_final(j)))
                else:
                    defer(i + 34, lambda j=j, h=h: epi2b(j, h))
            if hf == 1 and kt == 7 and fifo:
                cur[0] = fifo.pop(0)
                c4_half(cur[0][0], cur[0][1], 0)
            elif hf == 1 and kt == 8 and cur[0] is not None:
                jb, q = cur[0]
                cur[0] = None
                c4_half(jb, q, 1)
                c4_tail(jb, q)
                defer(i + 20, lambda jb=jb, q=q: do_pre(jb + 1, q))
        keys = sorted(deferred)
        flush_bank = [0]

        def flush_tile(jb, q):
            do_pre(jb, q)
            b0 = flush_bank[0] % 3
            flush_bank[0] += 1
            c4_half(jb, q, 0, bk=b0)
            c4_half(jb, q, 1, bk=3)
            c4_tail(jb, q)
            do_pre(jb + 1, q)

        first = True
        for k in keys:
            for fn in deferred[k]:
                fn()
            if first:
                first = False
                while fifo:
                    flush_tile(*fifo.pop(0))
        while fifo:
            flush_tile(*fifo.pop(0))

    sc.finalize()
    final_waits = [(k, v) for k, v in sc.dma_counts.items() if k.startswith("ys") or k.startswith("ws")]

    with contextlib.ExitStack() as es:
        sems = {}
        for k in semkeys:
            sems[k] = es.enter_context(nc.semaphore("e_" + k[1]))
        for name in sorted(dma_sem_names):
            sems[name] = es.enter_context(nc.semaphore("d_" + name))
        es.enter_context(nc.allow_low_precision("bf16 matmul operands, fp32 accumulation"))
        block = es.enter_context(nc.Block())

        @block.tensor
        def _(e):
            sc.emit("pe", e, sems)

        @block.scalar
        def _(e):
            sc.emit("act", e, sems)

        @block.vector
        def _(e):
            sc.emit("dve", e, sems)

        @block.gpsimd
        def _(e):
            sc.emit("pool", e, sems)

        @block.sync
        def _(e):
            sc.emit("sp", e, sems)
            for k, v in final_waits:
                e.wait_ge(sems[k], v)
    return nc


def _weight_groups(w_in, w_mem_kv, w_o):
    w = np.asarray(w_in, np.float32)[0]
    cols = []
    for h in range(4):
        base = np.arange(128)
        c, d = base // 64, base % 64
        dsw = np.where(d < 8, d + 8, np.where(d < 16, d - 8, d))
        q = h * 128 + base
        qp = h * 128 + c * 64 + dsw
        cols.append(np.concatenate([q, qp, 512 + q, 512 + qp]))
    for i in range(2):
        r = np.arange(128) + 128 * i
        cols.append(np.concatenate([2304 + r, 2560 + r, 2816 + r, 2048 + r]))
    cols.append(np.arange(1024, 1536))
    cols.append(np.arange(1536, 2048))
    cols.append(np.arange(3072, 3584))
    mats = [w[:, c] for c in cols]
    mats.append(np.asarray(w_mem_kv, np.float32)[0])
    wo = np.asarray(w_o, np.float32)[0]
    mats.append(wo[:, 0:512])
    mats.append(wo[:, 512:1024])
    out = np.empty((NG, 128, 8, 512), np.float32)
    for g, m in enumerate(mats):
        out[g] = m.reshape(8, 128, 512).transpose(1, 0, 2)
    return out.reshape(NG, 128, 4096)


def _perm_matrix():
    p = np.zeros((128, 128), np.float32)
    for r in range(128):
        d = r % 64
        k = r + 8 if d < 8 else (r - 8 if d < 16 else r)
        p[k, r] = 1.0
    return p


def _rope_tables():
    inv_freq = np.float64(ROPE_THETA) ** (-np.arange(0, 16, 2, dtype=np.float64) / 16.0)
    ang = np.arange(S, dtype=np.float64)[None, :] * inv_freq[:, None]
    cos, sin = np.cos(ang).astype(np.float32), np.sin(ang).astype(np.float32)
    ct = np.ones((128, S), np.float32)
    st = np.zeros((128, S), np.float32)
    for r in range(128):
        d = r % 64
        if d < 16:
            ct[r] = cos[d % 8]
            st[r] = -sin[d % 8] if d < 8 else sin[d % 8]
    return ct, st


_PROG = {}


def _run(x_all, mem_all, consts, ncores, nseq):
    if nseq not in _PROG:
        _PROG[nseq] = build_program(nseq)
    nc = _PROG[nseq]
    in_maps = []
    for c in range(ncores):
        m = dict(consts)
        m["x"] = np.ascontiguousarray(x_all[c * nseq:(c + 1) * nseq])
        m["mem"] = np.ascontiguousarray(mem_all[c * nseq:(c + 1) * nseq])
        in_maps.append(m)
    res = run_bass_kernel_spmd(nc, in_maps, core_ids=list(range(ncores)))
    return np.concatenate([np.asarray(r["y"]) for r in res.results], axis=0)


def _consts(in_ln_g, in_ln_b, w_in, w_mem_kv, lam_q1, lam_k1, lam_q2, lam_k2, subln_g, conv_w, w_o, ln_g, ln_b):
    ct, st = _rope_tables()
    cwh = np.asarray(conv_w, np.float32)[0].T.reshape(2, 128, 3).transpose(1, 0, 2).reshape(128, 6)
    return {
        "wsrc": _weight_groups(w_in, w_mem_kv, w_o),
        "lnp": np.stack([np.asarray(a, np.float32).reshape(D) for a in (in_ln_g, in_ln_b, ln_g, ln_b)]),
        "ctab": ct, "stab": st,
        "ident": np.eye(128, dtype=np.float32),
        "permm": _perm_matrix(),
        "cw": np.ascontiguousarray(cwh),
        "subg": np.asarray(subln_g, np.float32).reshape(128),
        "lnfm": np.ascontiguousarray(np.concatenate([np.asarray(a, np.float32).reshape(8, 128).T for a in (in_ln_g, in_ln_b)], axis=1)),
        "lamv": np.concatenate([np.asarray(a, np.float32).reshape(64) for a in (lam_q1, lam_k1, lam_q2, lam_k2)]),
    }


def kernel(x_prompt, x_sample, mem_prompt, mem_sample, in_ln_g, in_ln_b, w_in, w_mem_kv,
           lam_q1, lam_k1, lam_q2, lam_k2, subln_g, conv_w, w_o, ln_g, ln_b):
    x_prompt = np.asarray(x_prompt, np.float32)
    x_sample = np.asarray(x_sample, np.float32)
    nb_p = x_prompt.shape[0]
    x_all = np.concatenate([x_prompt, x_sample], axis=0)
    mem_all = np.concatenate([np.asarray(mem_prompt, np.float32), np.asarray(mem_sample, np.float32)], axis=0)
    consts = _consts(in_ln_g, in_ln_b, w_in, w_mem_kv, lam_q1, lam_k1, lam_q2, lam_k2,
                     subln_g, conv_w, w_o, ln_g, ln_b)
    y = _run(x_all, mem_all, consts, NCORES, SEQ_PER_CORE)
    return (np.ascontiguousarray(y[:nb_p]), np.ascontiguousarray(y[nb_p:]))
```

```python
import contextlib
import math
import numpy as np
import concourse.bass as bass
import concourse.mybir as mybir
from concourse.bass_utils import run_bass_kernel_spmd

F32 = mybir.dt.float32
BF16 = mybir.dt.bfloat16
AF = mybir.ActivationFunctionType
ALU = mybir.AluOpType
AX = mybir.AxisListType

D = 1024
S = 2048
NT = 16
NB = 4
NMEM = 256
NCORES = 8
SEQ_PER_CORE = 6
NG = 12
LN_EPS = 1e-5
ALPHA = 2.0 ** 0.25
LAM_INIT = 0.2
ROPE_THETA = 500000.0
CELL = 64


class Op:
    __slots__ = ("eng", "fn", "deps", "idx", "sig", "cnt", "sem", "dma")

    def __init__(self, eng, fn, sem):
        self.eng = eng
        self.fn = fn
        self.deps = set()
        self.sig = False
        self.cnt = 0
        self.sem = sem
        self.dma = sem is not None


class Sched:
    def __init__(self):
        self.engs = {"pe": [], "act": [], "dve": [], "pool": [], "sp": []}
        self.cells = {}
        self.dma_counts = {}

    @staticmethod
    def _cells_of(ap):
        name = ap.tensor.name
        if name not in ("arena", "ps"):
            return ()
        esz = 2 if ap.dtype == BF16 else 4
        pairs = ap.ap
        row = pairs[0][0]
        start = int(ap.offset) % row if row > 0 else int(ap.offset)
        ext = 1
        for st, cn in pairs[1:]:
            ext += (cn - 1) * st
        b0 = start * esz
        b1 = (start + ext) * esz
        c0 = b0 // (CELL * 4)
        c1 = (b1 - 1) // (CELL * 4)
        return [(name, c) for c in range(c0, c1 + 1)]

    def _dep(self, o, p, kind):
        if p is o:
            return
        if p.dma:
            o.deps.add(p)
            return
        if p.eng == o.eng:
            if o.dma:
                o.deps.add(p)
                return
            if o.eng == "pe":
                return
            o.deps.add(p)
            return
        o.deps.add(p)

    def op(self, eng, fn, outs=(), ins=(), xr=(), xw=(), sem=None):
        o = Op(eng, fn, sem)
        o.idx = len(self.engs[eng])
        self.engs[eng].append(o)
        rc = list(xr)
        wc = list(xw)
        for a in ins:
            rc.extend(self._cells_of(a))
        for a in outs:
            wc.extend(self._cells_of(a))
        cells = self.cells
        for r in rc:
            c = cells.get(r)
            if c is not None and c[0] is not None:
                self._dep(o, c[0], "raw")
        for r in wc:
            c = cells.get(r)
            if c is not None:
                if c[0] is not None:
                    self._dep(o, c[0], "waw")
                for rd in c[1].values():
                    self._dep(o, rd, "war")
        key = ("dma", id(o)) if o.dma else eng
        for r in rc:
            c = cells.get(r)
            if c is None:
                c = cells[r] = [None, {}]
            c[1][key] = o
        for r in wc:
            cells[r] = [o, {}]
        if o.dma:
            n = self.dma_counts.get(sem, 0) + 16
            self.dma_counts[sem] = n
            o.cnt = n
        return o

    def finalize(self):
        for ops in self.engs.values():
            for o in ops:
                for p in o.deps:
                    if not p.dma:
                        p.sig = True
        for ops in self.engs.values():
            n = 0
            for o in ops:
                if not o.dma and o.sig:
                    n += 1
                    o.cnt = n

    def emit(self, eng_name, eng, sems):
        waited = {}
        for o in self.engs[eng_name]:
            need = {}
            for p in o.deps:
                key = p.sem if p.dma else ("eng", p.eng)
                if waited.get(key, 0) >= p.cnt:
                    continue
                if need.get(key, 0) < p.cnt:
                    need[key] = p.cnt
            for key, cnt in need.items():
                eng.wait_ge(sems[key], cnt)
                waited[key] = cnt
            inst = o.fn(eng)
            if o.dma:
                inst.then_inc(sems[o.sem], 16)
            elif o.sig:
                inst.then_inc(sems[("eng", o.eng)], 1)


def build_program(nseq):
    nc = bass.Bass("TRN2", target_bir_lowering=False)
    x_d = nc.dram_tensor("x", [nseq, S, D], F32, kind="ExternalInput").ap()
    mem_d = nc.dram_tensor("mem", [nseq, NMEM, D], F32, kind="ExternalInput").ap()
    wsrc_d = nc.dram_tensor("wsrc", [NG, 128, 4096], F32, kind="ExternalInput").ap()
    lnp_d = nc.dram_tensor("lnp", [4, D], F32, kind="ExternalInput").ap()
    ctab_d = nc.dram_tensor("ctab", [128, S], F32, kind="ExternalInput").ap()
    stab_d = nc.dram_tensor("stab", [128, S], F32, kind="ExternalInput").ap()
    ident_d = nc.dram_tensor("ident", [128, 128], F32, kind="ExternalInput").ap()
    permm_d = nc.dram_tensor("permm", [128, 128], F32, kind="ExternalInput").ap()
    cw_d = nc.dram_tensor("cw", [128, 6], F32, kind="ExternalInput").ap()
    sg_d = nc.dram_tensor("subg", [128], F32, kind="ExternalInput").ap()
    lam_d = nc.dram_tensor("lamv", [256], F32, kind="ExternalInput").ap()
    lnfm_d = nc.dram_tensor("lnfm", [128, 16], F32, kind="ExternalInput").ap()
    y_d = nc.dram_tensor("y", [nseq, S, D], F32, kind="ExternalOutput").ap()
    wsc_d = nc.dram_tensor("wsc", [NG, 128, 4096], BF16, kind="Internal").ap()
    rsc_d = nc.dram_tensor("rsc", [2, 512], F32, kind="Internal").ap()

    off = [0]

    def alloc(words):
        words = (words + CELL - 1) // CELL * CELL
        o = off[0]
        off[0] += words
        return o

    o_gin, o_bin, o_g2, o_b2 = alloc(1024), alloc(1024), alloc(1024), alloc(1024)
    o_ctab, o_stab = alloc(2048), alloc(2048)
    o_idb = alloc(64)
    o_ones = alloc(64)
    o_permb = alloc(64)
    o_sgp1 = alloc(1)
    o_lnfm = alloc(16)
    o_sgp = alloc(128)
    o_cw = alloc(8)
    o_lamv = alloc(256)
    o_lprod = alloc(128)
    o_ls = alloc(2)
    o_le = alloc(2)
    o_nl = alloc(1)
    o_stats = alloc(32)
    o_st12 = alloc(12)
    o_mv = alloc(2)
    o_st12b = alloc(12)
    o_mvb = alloc(2)
    o_st2 = alloc(2)
    o_rs = alloc(8)
    o_rsl = alloc(4)
    o_ss = alloc(4)
    o_rinv = alloc(4)
    o_rsm = alloc(16)
    o_mhalf = alloc(4)
    o_smin = alloc(4)
    o_smout = alloc(4)
    o_epsc = alloc(4)
    o_eps128 = alloc(4)
    o_tmpv = alloc(4)
    o_wg = [alloc(2048), alloc(2048)]
    o_QT = alloc(4096)
    o_KT = alloc(4096)
    o_KT1 = alloc(4096)
    o_VA = alloc(4160)
    o_gate = alloc(4096)
    o_ocv = alloc(2048)
    o_omx = alloc(2048)
    o_kmx = alloc(256)
    o_vmx = alloc(264)
    shared0 = off[0]
    o_hT = alloc(8192)
    tmp0 = off[0]
    o_xt = [alloc(1024), alloc(1024), alloc(1024)]
    o_hb = [alloc(512) for _ in range(4)]
    endA = off[0]
    off[0] = tmp0
    o_memb = alloc(1024)
    o_memT = alloc(1024)
    endM = off[0]
    off[0] = tmp0
    o_tA, o_tB = alloc(512), alloc(512)
    o_qtmp = [alloc(256) for _ in range(4)]
    endR = off[0]
    off[0] = tmp0
    o_csb, o_sgc, o_bgb, o_u, o_yc = alloc(512), alloc(512), alloc(512), alloc(2052), alloc(512)
    endC = off[0]
    off[0] = tmp0
    o_gtmp = alloc(512)
    off[0] = tmp0
    o_qmxb = alloc(512)
    o_gmxb = alloc(1024)
    o_Em = [alloc(256) for _ in range(4)]
    o_omxf = alloc(1024)
    o_ofm = alloc(512)
    endG8 = off[0]
    off[0] = shared0
    o_E = [alloc(256) for _ in range(4)]
    o_mix = [alloc(1024), alloc(1024)]
    o_rcp = [alloc(512), alloc(512)]
    o_tO = [alloc(512), alloc(512)]
    o_Of = alloc(512)
    o_rscr = alloc(512)
    o_pp = alloc(512)
    o_sqb = alloc(256)
    o_xr = [alloc(1024) for _ in range(4)]
    endP = off[0]
    AW = max(endA, endM, endR, endC, endG8, endP)

    arena = nc.alloc_sbuf_tensor("arena", [128, AW], F32)
    ps = nc.alloc_psum_tensor("ps", [128, 4096], F32)

    def f32v(o, n):
        return arena[:, o:o + n]

    def bfv(o, nwords):
        return arena[:, o:o + nwords].bitcast(BF16)

    def bank(b, n=512):
        return ps[:, b * 512:b * 512 + n]

    def bank_bf(b):
        return ps[:, b * 512:(b + 1) * 512].bitcast(BF16)

    gin, bin_, g2, b2 = f32v(o_gin, 1024), f32v(o_bin, 1024), f32v(o_g2, 1024), f32v(o_b2, 1024)
    ctab, stab = f32v(o_ctab, 2048), f32v(o_stab, 2048)
    idb = bfv(o_idb, 64)
    onesb = bfv(o_ones, 64)
    permb = bfv(o_permb, 64)
    sgp1 = f32v(o_sgp1, 1)
    lnfm = f32v(o_lnfm, 16)
    sgp = f32v(o_sgp, 128)
    cw = f32v(o_cw, 6)
    lamv = f32v(o_lamv, 256)
    lprod = f32v(o_lprod, 128)
    ls, le, nl = f32v(o_ls, 2), f32v(o_le, 2), f32v(o_nl, 1)
    stats = f32v(o_stats, 32).rearrange("p (t k) -> p t k", k=2)
    st12, mv = f32v(o_st12, 12), f32v(o_mv, 2)
    st12b, mvb, st2 = f32v(o_st12b, 12), f32v(o_mvb, 2), f32v(o_st2, 2)
    rs = f32v(o_rs, 8).rearrange("p (q c) -> p q c", c=2)
    rsl, ss, rinv = f32v(o_rsl, 4), f32v(o_ss, 4), f32v(o_rinv, 4)
    rsm = f32v(o_rsm, 16).rearrange("p (q h) -> p q h", h=4)
    sm_in, sm_out = f32v(o_smin, 4), f32v(o_smout, 4)
    mhalf, epsc, eps128, tmpv = f32v(o_mhalf, 4), f32v(o_epsc, 4), f32v(o_eps128, 4), f32v(o_tmpv, 4)
    wg = [bfv(o, 2048).rearrange("p (c n) -> p c n", n=512) for o in o_wg]
    wg_flat = [bfv(o, 2048) for o in o_wg]
    QT = bfv(o_QT, 4096).rearrange("p (h t) -> p h t", t=S)
    KT = bfv(o_KT, 4096).rearrange("p (h t) -> p h t", t=S)
    KT1 = bfv(o_KT1, 4096).rearrange("p (h t) -> p h t", t=S)
    VA = bfv(o_VA, 4160).rearrange("p (t h e) -> p t h e", h=4, e=130)
    gateT = bfv(o_gate, 4096).rearrange("p (h t) -> p h t", t=S)
    ocvT = bfv(o_ocv, 2048).rearrange("p (i t) -> p i t", t=S)
    omxT = bfv(o_omx, 2048).rearrange("p (i t) -> p i t", t=S)
    KmxT = bfv(o_kmx, 256).rearrange("p (i t) -> p i t", t=256)
    Vmx = bfv(o_vmx, 264).rearrange("p (t h e) -> p t h e", h=4, e=66)
    hT = bfv(o_hT, 8192).rearrange("p (b c t) -> p b c t", c=8, t=512)
    xt = [f32v(o, 1024) for o in o_xt]
    hb = [bfv(o, 512) for o in o_hb]
    memb = bfv(o_memb, 1024).rearrange("p (t d) -> p t d", d=1024)
    memT = bfv(o_memT, 1024).rearrange("p (c t) -> p c t", t=256)
    tA, tB = f32v(o_tA, 512), f32v(o_tB, 512)
    qtmp = [bfv(o, 256) for o in o_qtmp]
    csb, sgc, bgb, yc = f32v(o_csb, 512), f32v(o_sgc, 512), f32v(o_bgb, 512), f32v(o_yc, 512)
    u = f32v(o_u, 2050)
    gtmp = f32v(o_gtmp, 512)
    qmxb = bfv(o_qmxb, 512).rearrange("p (i t) -> p i t", t=512)
    gmxb = f32v(o_gmxb, 1024).rearrange("p (q n) -> p q n", n=256)
    Em = [bfv(o, 256) for o in o_Em]
    omxf = f32v(o_omxf, 1024)
    ofm = bfv(o_ofm, 512).rearrange("p (q n) -> p q n", n=256)
    E = [bfv(o, 256) for o in o_E]
    mix = [bfv(o, 1024).rearrange("p (h t) -> p h t", t=512) for o in o_mix]
    rcp = [f32v(o, 512) for o in o_rcp]
    tO = [f32v(o, 512) for o in o_tO]
    Of = f32v(o_Of, 512)
    rscr = f32v(o_rscr, 512)
    pp = f32v(o_pp, 512)
    sqb = bfv(o_sqb, 256)
    xr = [f32v(o, 1024) for o in o_xr]

    sc = Sched()
    semkeys = [("eng", e) for e in ("pe", "act", "dve", "pool")]
    dma_sem_names = set()

    def dsem(name):
        dma_sem_names.add(name)
        return name

    def dma_sp(out, in_, sem, xr_=(), xw_=()):
        sc.op("sp", lambda e: e.dma_start(out=out, in_=in_), outs=[out], ins=[in_],
              xr=xr_, xw=xw_, sem=dsem(sem))

    def dma_pool(out, in_, sem, xr_=(), xw_=()):
        sc.op("pool", lambda e: e.dma_start(out=out, in_=in_), outs=[out], ins=[in_],
              xr=xr_, xw=xw_, sem=dsem(sem))

    def mm(out, lhsT, rhs, start, stop, skip=False):
        sc.op("pe", lambda e: e.matmul(out, lhsT, rhs, start=start, stop=stop, skip_group_check=skip),
              outs=[out], ins=[lhsT, rhs])

    def tr(out, in_):
        sc.op("pe", lambda e: e.transpose(out, in_, idb), outs=[out], ins=[in_, idb])

    def act(out, in_, func, bias=0.0, scale=1.0, extra=()):
        ins = [in_] + [a for a in (bias, scale) if not isinstance(a, float)] + list(extra)
        sc.op("act", lambda e: e.activation(out, in_, func, bias=bias, scale=scale),
              outs=[out], ins=ins)

    def tt(eng, out, in0, in1, op, extra=()):
        sc.op(eng, lambda e: e.tensor_tensor(out, in0, in1, op), outs=[out], ins=[in0, in1] + list(extra))

    def ts(eng, out, in0, s1, s2, op0, op1=None):
        ins = [in0] + [a for a in (s1, s2) if a is not None and not isinstance(a, float)]
        if op1 is None:
            sc.op(eng, lambda e: e.tensor_scalar(out, in0, s1, s2, op0), outs=[out], ins=ins)
        else:
            sc.op(eng, lambda e: e.tensor_scalar(out, in0, s1, s2, op0, op1), outs=[out], ins=ins)

    def stt(eng, out, in0, scalar, in1, op0, op1):
        ins = [in0, in1] + ([scalar] if not isinstance(scalar, float) else [])
        sc.op(eng, lambda e: e.scalar_tensor_tensor(out, in0, scalar, in1, op0, op1),
              outs=[out], ins=ins)

    def cp(eng, out, in_):
        if eng == "act":
            act(out, in_, AF.Copy)
        else:
            sc.op(eng, lambda e: e.tensor_copy(out, in_), outs=[out], ins=[in_])

    def memset(eng, out, val):
        sc.op(eng, lambda e: e.memset(out, val), outs=[out])

    def bc(ap2d, shape):
        return ap2d.unsqueeze(len(ap2d.shape)).broadcast_to(shape)

    for i, dst in enumerate((gin, bin_, g2, b2)):
        dma_sp(dst, lnp_d[i].partition_broadcast(128), "c%d" % i)
    dma_sp(ctab, ctab_d, "c4")
    dma_sp(stab, stab_d, "c5")
    dma_sp(cw, cw_d, "c6")
    dma_sp(sgp, sg_d.partition_broadcast(128), "c7")
    dma_sp(sgp1, sg_d.rearrange("(p o) -> p o", o=1), "c10")
    dma_sp(lnfm, lnfm_d, "c11")
    dma_sp(lamv, lam_d.partition_broadcast(128), "c8")
    dma_pool(idb, ident_d, "c9")
    dma_pool(permb, permm_d, "c12")
    ts("dve", sgp1, sgp1, (1.0 - LAM_INIT) * math.sqrt(128.0), None, ALU.mult)
    memset("pool", onesb, 1.0)
    lv = lamv.rearrange("p (a b d) -> p a b d", a=2, b=2)
    tt("dve", lprod.rearrange("p (a d) -> p a d", d=64), lv[:, :, 0, :], lv[:, :, 1, :], ALU.mult)
    sc.op("dve", lambda e: e.tensor_reduce(ls, lprod.rearrange("p (a d) -> p a d", d=64), AX.X, ALU.add),
          outs=[ls], ins=[lprod])
    act(le, ls, AF.Exp)
    stt("dve", nl, le[:, 0:1], -1.0, le[:, 1:2], ALU.mult, ALU.add)
    ts("dve", nl, nl, -LAM_INIT, None, ALU.add)
    memset("pool", mhalf, -0.5)
    memset("pool", epsc, LN_EPS)
    memset("pool", eps128, 128.0 * LN_EPS)
    memset("pool", KT[64:128, :, :], 0.0)
    memset("pool", KT1[0:64, :, :], 0.0)
    memset("pool", VA[:, :, :, 128:130], 1.0)
    memset("pool", Vmx[:, :, :, 64:66], 1.0)
    for g in range(NG):
        sl = g % 2
        dma_pool(wg_flat[sl], wsrc_d[g], "wc%d" % sl)
        dma_sp(wsc_d[g], wg_flat[sl], "ws%d" % sl, xw_=[("wsc", g)])

    wslot = [0]

    def load_w(g):
        sl = wslot[0] % 2
        wslot[0] += 1
        dma_sp(wg_flat[sl], wsc_d[g], "wl%d" % sl, xr_=[("wsc", g)])
        return sl

    psrot = [0]

    def next_pair():
        p = psrot[0] % 2
        psrot[0] += 1
        return 2 * p, 2 * p + 1

    def fm_tile(bk, sl, i, blk):
        for c in range(8):
            mm(bank(bk), wg[sl][:, c, i * 128:(i + 1) * 128], hT[:, blk, c, :], c == 0, c == 7)

    def ln_stats(src, st_, mv_, rstd_out, nmr_out):
        sc.op("dve", lambda e: e.bn_stats(st_[:, 0:6], src[:, 0:512]), outs=[st_[:, 0:6]], ins=[src[:, 0:512]])
        sc.op("dve", lambda e: e.bn_stats(st_[:, 6:12], src[:, 512:1024]), outs=[st_[:, 6:12]], ins=[src[:, 512:1024]])
        sc.op("dve", lambda e: e.bn_aggr(mv_, st_), outs=[mv_], ins=[st_])
        tt("pool", tmpv[:, 0:1], mv_[:, 1:2], epsc[:, 0:1], ALU.add)
        tt("pool", rstd_out, tmpv[:, 0:1], mhalf[:, 0:1], ALU.pow)
        stt("dve", nmr_out, mv_[:, 0:1], -1.0, rstd_out, ALU.mult, ALU.mult)

    ystores = {}
    ACC = ps[:, 4 * 512:8 * 512].rearrange("p (q n) -> p q n", n=512)

    for s in range(nseq):
        slkv = load_w(9)
        dma_pool(memb, mem_d[s].rearrange("(t p) d -> p t d", p=128), "mem")
        for t in range(2):
            pt = bank_bf(7)
            for c in range(8):
                tr(pt[:, c * 128:(c + 1) * 128], memb[:, t, c * 128:(c + 1) * 128])
            cp("act", memT[:, :, t * 128:(t + 1) * 128], pt.rearrange("p (c t) -> p c t", t=128))
        for i in range(2):
            bk = next_pair()[0]
            for c in range(8):
                mm(bank(bk, 256), wg[slkv][:, c, i * 128:(i + 1) * 128], memT[:, c, :], c == 0, c == 7)
            cp("act", KmxT[:, i, :], bank(bk, 256))
        for t in range(2):
            bk = next_pair()[0]
            for c in range(8):
                mm(bank(bk, 256), memT[:, c, t * 128:(t + 1) * 128], wg[slkv][:, c, 256:512], c == 0, c == 7)
            cp("dve", Vmx[:, t, :, 0:64], bank(bk, 256).rearrange("p (h e) -> p h e", e=64))

        sl_next = load_w(0)
        def a_front(p):
            for k in range(2):
                t = 2 * p + k
                xs = xt[t % 3]
                dma_sp(xs, x_d[s, t * 128:(t + 1) * 128, :], "x%d" % (t % 3))
                ln_stats(xs, st12, mv, stats[:, t, 0:1], stats[:, t, 1:2])
                act(hb[(p % 2) * 2 + k], xs, AF.Identity, bias=stats[:, t, 1:2], scale=stats[:, t, 0:1])

        def a_region(p):
            b0 = 4 + 2 * (p % 2)
            return b0, ps[:, b0 * 512:(b0 + 2) * 512].bitcast(BF16)

        def a_mid(p):
            b0, region = a_region(p)
            for c in range(8):
                for k in range(2):
                    tr(region[:, c * 256 + k * 128:c * 256 + (k + 1) * 128],
                       hb[(p % 2) * 2 + k][:, c * 128:(c + 1) * 128])

        def a_back(p):
            b0, region = a_region(p)
            t0 = 2 * p
            col0 = (t0 % 4) * 128
            for c in range(8):
                act(hT[:, t0 // 4, c, col0:col0 + 256], region[:, c * 256:(c + 1) * 256], AF.Identity,
                    bias=lnfm[:, 8 + c:9 + c], scale=lnfm[:, c:c + 1], extra=[bank_bf(b0 + c // 4)])

        npair = NT // 2
        a_front(0)
        a_mid(0)
        for p in range(1, npair):
            a_front(p)
            a_mid(p)
            a_back(p - 1)
        a_back(npair - 1)

        for g in range(9):
            sl = sl_next
            if g < 8:
                sl_next = load_w(g + 1)
            if g < 4:
                h = g
                for blk in range(NB + 1):
                    if blk < NB:
                        sb = 4 * (blk % 2)
                        fm_tile(sb, sl, 0, blk)
                        fm_tile(sb + 2, sl, 2, blk)
                        cp("act", qtmp[2 * (blk % 2)], bank(sb))
                        cp("act", qtmp[2 * (blk % 2) + 1], bank(sb + 2))
                    if blk >= 1:
                        pb_ = blk - 1
                        sb = 4 * (pb_ % 2)
                        cs = slice(pb_ * 512, (pb_ + 1) * 512)
                        mm(bank(sb + 1), permb, qtmp[2 * (pb_ % 2)], True, True)
                        mm(bank(sb + 3), permb, qtmp[2 * (pb_ % 2) + 1], True, True)
                        tt("dve", tA, bank(sb), ctab[:, cs], ALU.mult, extra=[qtmp[2 * (pb_ % 2)]])
                        tt("dve", tB, bank(sb + 1), stab[:, cs], ALU.mult)
                        tt("dve", QT[:, h, cs], tA, tB, ALU.add)
                        tt("dve", tA, bank(sb + 2), ctab[:, cs], ALU.mult, extra=[qtmp[2 * (pb_ % 2) + 1]])
                        tt("dve", tB, bank(sb + 3), stab[:, cs], ALU.mult)
                        tt("dve", KT[0:64, h, cs], tA[0:64, :], tB[0:64, :], ALU.add)
                        tt("dve", KT1[64:128, h, cs], tA[64:128, :], tB[64:128, :], ALU.add)
            elif g < 6:
                i = g - 4
                memset("pool", u[:, 0:1], 0.0)
                memset("pool", u[:, 2049:2050], 0.0)
                for blk in range(NB):
                    ba, bb = next_pair()
                    fm_tile(ba, sl, 0, blk)
                    cp("act", csb, bank(ba))
                    fm_tile(bb, sl, 1, blk)
                    tt("dve", u[:, 1 + blk * 512:1 + (blk + 1) * 512], bank(bb), csb, ALU.mult)
                for blk in range(NB):
                    ba, bb = next_pair()
                    fm_tile(ba, sl, 2, blk)
                    act(sgc, bank(ba), AF.Silu)
                    fm_tile(bb, sl, 3, blk)
                    tt("dve", bgb, bank(bb), sgc, ALU.mult)
                    b0 = blk * 512
                    ts("dve", yc, u[:, b0:b0 + 512], cw[:, 3 * i:3 * i + 1], None, ALU.mult)
                    stt("dve", yc, u[:, b0 + 1:b0 + 513], cw[:, 3 * i + 1:3 * i + 2], yc, ALU.mult, ALU.add)
                    stt("dve", yc, u[:, b0 + 2:b0 + 514], cw[:, 3 * i + 2:3 * i + 3], yc, ALU.mult, ALU.add)
                    tt("dve", ocvT[:, i, b0:b0 + 512], yc, bgb, ALU.mult)
            elif g == 6:
                for t in range(NT):
                    bk = next_pair()[t % 2]
                    for c in range(8):
                        mm(bank(bk), hT[:, t // 4, c, (t % 4) * 128:(t % 4 + 1) * 128], wg[sl][:, c, :], c == 0, c == 7)
                    cp("act", VA[:, t, :, 0:128], bank(bk).rearrange("p (h e) -> p h e", e=128))
            elif g == 7:
                for i in range(4):
                    for blk in range(NB):
                        bk = next_pair()[blk % 2]
                        fm_tile(bk, sl, i, blk)
                        act(gateT[:, i, blk * 512:(blk + 1) * 512], bank(bk), AF.Silu)
            else:
                for blk in range(NB):
                    for i in range(2):
                        bk = next_pair()[i]
                        fm_tile(bk, sl, i, blk)
                        cp("dve", qmxb[:, i, :], bank(bk))
                    for q in range(4):
                        bk = next_pair()[q % 2]
                        for c in range(8):
                            mm(bank(bk, 256), hT[:, blk, c, q * 128:(q + 1) * 128], wg[sl][:, c, 256:512], c == 0, c == 7)
                        act(gmxb[:, q, :], bank(bk, 256), AF.Silu)
                    its = [(p, mk) for p in range(2) for mk in range(2)]

                    def mx_s(n):
                        p, mk = its[n]
                        b0 = 2 * (n % 2)
                        mm(bank(b0), KmxT[0:64, p, mk * 128:(mk + 1) * 128], qmxb[0:64, p, :], True, True)
                        mm(bank(b0 + 1), KmxT[64:128, p, mk * 128:(mk + 1) * 128], qmxb[64:128, p, :], True, True)

                    mx_s(0)
                    for n, (p, mk) in enumerate(its):
                        if n + 1 < len(its):
                            mx_s(n + 1)
                        b0 = 2 * (n % 2)
                        act(Em[b0], bank(b0), AF.Exp, scale=0.125)
                        act(Em[b0 + 1], bank(b0 + 1), AF.Exp, scale=0.125)
                        for q in range(4):
                            for e in range(2):
                                hd = 2 * p + e
                                mm(ACC[:, q, hd * 66:hd * 66 + 65], Em[b0 + e][:, q * 128:(q + 1) * 128],
                                   Vmx[:, mk, hd, 0:65], mk == 0 and hd == 0, mk == 1, skip=True)
                    accm = ACC[:, :, 0:264].rearrange("p q (h e) -> p q h e", e=66)
                    sc.op("dve", lambda e: e.reciprocal(rsm, accm[:, :, :, 64]), outs=[rsm], ins=[accm[:, :, :, 64]])
                    for q in range(4):
                        tt("dve", omxf[:, q * 256:(q + 1) * 256].rearrange("p (h e) -> p h e", e=64),
                           accm[:, q, :, 0:64], bc(rsm[:, q, :], [128, 4, 64]), ALU.mult)
                    tt("dve", ofm, omxf.rearrange("p (q n) -> p q n", n=256), gmxb, ALU.mult)
                    pt = bank_bf(2)
                    for i in range(2):
                        for q in range(4):
                            tr(pt[:, (i * 4 + q) * 128:(i * 4 + q + 1) * 128], ofm[:, q, i * 128:(i + 1) * 128])
                    cp("dve", omxT[:, :, blk * 512:(blk + 1) * 512], pt.rearrange("p (i t) -> p i t", t=512))

        steps = [(j, h, kt, hf) for j in range(NB) for h in range(4) for kt in range(NT) for hf in range(2)]
        SPH = 2 * NT

        def s_step(i):
            j, h, kt, hf = steps[i]
            ks = slice(kt * 128, (kt + 1) * 128)
            q0 = j * 512 + hf * 256
            r = i % 3
            mm(ps[:, r * 512:r * 512 + 256], KT[:, h, ks], QT[:, h, q0:q0 + 256], True, True)
            mm(ps[:, r * 512 + 256:r * 512 + 512], KT1[:, h, ks], QT[:, h, q0:q0 + 256], True, True)

        def exp_step(i):
            act(E[i % 4], bank(i % 3), AF.Exp, scale=0.125)

        def av_step(i):
            j, h, kt, hf = steps[i]
            mm(bank(4 + hf), VA[:, kt, h, 0:128], E[i % 4], kt == 0, kt == NT - 1)
            mm(bank(6 + hf), onesb, E[i % 4], kt == 0, kt == NT - 1)

        def epi1(j, h):
            for hf in range(2):
                cp("dve", tO[hf], bank(4 + hf))
                cp("act", rcp[hf][:, 0:256], ps[:, (6 + hf) * 512 + 256:(6 + hf) * 512 + 512])
                cp("act", rcp[hf][:, 256:512], ps[:, (6 + hf) * 512:(6 + hf) * 512 + 256])

        def epi2(j, h):
            for hf in range(2):
                tt("dve", tO[hf], tO[hf], rcp[hf], ALU.mult)
            for hf in range(2):
                hs = slice(hf * 256, (hf + 1) * 256)
                stt("dve", Of[:, hs], tO[hf][:, 256:512], nl[:, 0:1], tO[hf][:, 0:256], ALU.mult, ALU.add)
                tt("dve", pp[:, hs], rcp[hf][:, 0:256], rcp[hf][:, 256:512], ALU.mult)
            stt("dve", pp, pp, 128.0 * LN_EPS, pp, ALU.mult, ALU.mult)
            tt("dve", sqb, Of, Of, ALU.mult)

        def epi2m(j, h):
            mm(bank(3), onesb, sqb, True, True)
            tt("dve", rscr, bank(3), pp, ALU.add)
            dma_sp(rsc_d[0:1, :], rscr[0:1, :], "r0", xw_=[("rsc", 0)])
            dma_sp(sm_in, rsc_d[0].rearrange("(p c) -> p c", c=4), "r1", xr_=[("rsc", 0)])
            tt("pool", sm_out, sm_in, mhalf, ALU.pow)
            dma_sp(rsc_d[1].rearrange("(p c) -> p c", c=4), sm_out, "r2", xw_=[("rsc", 1)])
            dma_sp(pp, rsc_d[1].partition_broadcast(128), "r3", xr_=[("rsc", 1)])

        def epi2b(j, h):
            tt("dve", rscr, pp, gateT[:, h, j * 512:(j + 1) * 512], ALU.mult)
            stt("dve", mix[j % 2][:, h, :], Of, sgp1[:, 0:1], rscr, ALU.mult, ALU.mult)

        def c4_pre(j, q):
            t = 4 * j + q
            xs = xr[q]
            dma_sp(xs, x_d[s, t * 128:(t + 1) * 128, :], "xr%d" % q)
            stt("dve", xs, xs, stats[:, t, 0:1], gin, ALU.mult, ALU.mult)
            stt("dve", xs, gin, stats[:, t, 1:2], xs, ALU.mult, ALU.add)
            tt("dve", xs, xs, bin_, ALU.add)

        def c4_half(j, q, half, bk=3):
            t = 4 * j + q
            xs = xr[q]
            mx = mix[j % 2]
            tsl = slice(t * 128, (t + 1) * 128)
            for c in range(8):
                if c < 4:
                    lhsT = mx[:, c, q * 128:(q + 1) * 128]
                elif c < 6:
                    lhsT = ocvT[:, c - 4, tsl]
                else:
                    lhsT = omxT[:, c - 6, tsl]
                mm(bank(bk), lhsT, wg[slo[half]][:, c, :], c == 0, c == 7)
            hs = slice(half * 512, (half + 1) * 512)
            stt("dve", xs[:, hs], xs[:, hs], ALPHA, bank(bk), ALU.mult, ALU.add)

        def c4_tail(j, q):
            t = 4 * j + q
            xs = xr[q]
            ln_stats(xs, st12b, mvb, st2[:, 0:1], st2[:, 1:2])
            stt("dve", xs, xs, st2[:, 0:1], g2, ALU.mult, ALU.mult)
            stt("dve", xs, g2, st2[:, 1:2], xs, ALU.mult, ALU.add)
            tt("dve", xs, xs, b2, ALU.add)
            dma_sp(y_d[s, t * 128:(t + 1) * 128, :], xs, "ys%d" % q)

        slo = [load_w(10), load_w(11)]
        for q in range(4):
            c4_pre(0, q)
        nsteps = len(steps)
        deferred = {}
        fifo = []
        pre_done = set((0, q) for q in range(4))

        def defer(at, fn):
            deferred.setdefault(at, []).append(fn)

        def mix_final(jb):
            for q in range(4):
                fifo.append((jb, q))

        def do_pre(jb, q):
            if jb < NB and (jb, q) not in pre_done:
                pre_done.add((jb, q))
                c4_pre(jb, q)

        cur = [None]
        s_step(0)
        s_step(1)
        for i in range(nsteps):
            j, h, kt, hf = steps[i]
            if i + 2 < nsteps:
                s_step(i + 2)
            exp_step(i)
            av_step(i)
            for fn in deferred.pop(i, []):
                fn()
            if kt == NT - 1 and hf == 1:
                epi1(j, h)
                defer(i + 5, lambda j=j, h=h: epi2(j, h))
                defer(i + 13, lambda j=j, h=h: epi2m(j, h))
                if h == 3:
                    defer(i + 34, lambda j=j, h=h: (epi2b(j, h), mix_final(j)))
                else:
                    defer(i + 34, lambda j=j, h=h: epi2b(j, h))
            if hf == 1 and kt == 7 and fifo:
                cur[0] = fifo.pop(0)
                c4_half(cur[0][0], cur[0][1], 0)
            elif hf == 1 and kt == 8 and cur[0] is not None:
                jb, q = cur[0]
                cur[0] = None
                c4_half(jb, q, 1)
                c4_tail(jb, q)
                defer(i + 20, lambda jb=jb, q=q: do_pre(jb + 1, q))
        keys = sorted(deferred)
        flush_bank = [0]

        def flush_tile(jb, q):
            do_pre(jb, q)
            b0 = flush_bank[0] % 3
            flush_bank[0] += 1
            c4_half(jb, q, 0, bk=b0)
            c4_half(jb, q, 1, bk=3)
            c4_tail(jb, q)
            do_pre(jb + 1, q)

        first = True
        for k in keys:
            for fn in deferred[k]:
                fn()
            if first:
                first = False
                while fifo:
                    flush_tile(*fifo.pop(0))
        while fifo:
            flush_tile(*fifo.pop(0))

    sc.finalize()
    final_waits = [(k, v) for k, v in sc.dma_counts.items() if k.startswith("ys") or k.startswith("ws")]

    with contextlib.ExitStack() as es:
        sems = {}
        for k in semkeys:
            sems[k] = es.enter_context(nc.semaphore("e_" + k[1]))
        for name in sorted(dma_sem_names):
            sems[name] = es.enter_context(nc.semaphore("d_" + name))
        es.enter_context(nc.allow_low_precision("bf16 matmul operands, fp32 accumulation"))
        block = es.enter_context(nc.Block())

        @block.tensor
        def _(e):
            sc.emit("pe", e, sems)

        @block.scalar
        def _(e):
            sc.emit("act", e, sems)

        @block.vector
        def _(e):
            sc.emit("dve", e, sems)

        @block.gpsimd
        def _(e):
            sc.emit("pool", e, sems)

        @block.sync
        def _(e):
            sc.emit("sp", e, sems)
            for k, v in final_waits:
                e.wait_ge(sems[k], v)
    return nc


def _weight_groups(w_in, w_mem_kv, w_o):
    w = np.asarray(w_in, np.float32)[0]
    cols = []
    for h in range(4):
        base = np.arange(128)
        c, d = base // 64, base % 64
        dsw = np.where(d < 8, d + 8, np.where(d < 16, d - 8, d))
        q = h * 128 + base
        qp = h * 128 + c * 64 + dsw
        cols.append(np.concatenate([q, qp, 512 + q, 512 + qp]))
    for i in range(2):
        r = np.arange(128) + 128 * i
        cols.append(np.concatenate([2304 + r, 2560 + r, 2816 + r, 2048 + r]))
    cols.append(np.arange(1024, 1536))
    cols.append(np.arange(1536, 2048))
    cols.append(np.arange(3072, 3584))
    mats = [w[:, c] for c in cols]
    mats.append(np.asarray(w_mem_kv, np.float32)[0])
    wo = np.asarray(w_o, np.float32)[0]
    mats.append(wo[:, 0:512])
    mats.append(wo[:, 512:1024])
    out = np.empty((NG, 128, 8, 512), np.float32)
    for g, m in enumerate(mats):
        out[g] = m.reshape(8, 128, 512).transpose(1, 0, 2)
    return out.reshape(NG, 128, 4096)


def _perm_matrix():
    p = np.zeros((128, 128), np.float32)
    for r in range(128):
        d = r % 64
        k = r + 8 if d < 8 else (r - 8 if d < 16 else r)
        p[k, r] = 1.0
    return p


def _rope_tables():
    inv_freq = np.float64(ROPE_THETA) ** (-np.arange(0, 16, 2, dtype=np.float64) / 16.0)
    ang = np.arange(S, dtype=np.float64)[None, :] * inv_freq[:, None]
    cos, sin = np.cos(ang).astype(np.float32), np.sin(ang).astype(np.float32)
    ct = np.ones((128, S), np.float32)
    st = np.zeros((128, S), np.float32)
    for r in range(128):
        d = r % 64
        if d < 16:
            ct[r] = cos[d % 8]
            st[r] = -sin[d % 8] if d < 8 else sin[d % 8]
    return ct, st


_PROG = {}


def _run(x_all, mem_all, consts, ncores, nseq):
    if nseq not in _PROG:
        _PROG[nseq] = build_program(nseq)
    nc = _PROG[nseq]
    in_maps = []
    for c in range(ncores):
        m = dict(consts)
        m["x"] = np.ascontiguousarray(x_all[c * nseq:(c + 1) * nseq])
        m["mem"] = np.ascontiguousarray(mem_all[c * nseq:(c + 1) * nseq])
        in_maps.append(m)
    res = run_bass_kernel_spmd(nc, in_maps, core_ids=list(range(ncores)))
    return np.concatenate([np.asarray(r["y"]) for r in res.results], axis=0)


def _consts(in_ln_g, in_ln_b, w_in, w_mem_kv, lam_q1, lam_k1, lam_q2, lam_k2, subln_g, conv_w, w_o, ln_g, ln_b):
    ct, st = _rope_tables()
    cwh = np.asarray(conv_w, np.float32)[0].T.reshape(2, 128, 3).transpose(1, 0, 2).reshape(128, 6)
    return {
        "wsrc": _weight_groups(w_in, w_mem_kv, w_o),
        "lnp": np.stack([np.asarray(a, np.float32).reshape(D) for a in (in_ln_g, in_ln_b, ln_g, ln_b)]),
        "ctab": ct, "stab": st,
        "ident": np.eye(128, dtype=np.float32),
        "permm": _perm_matrix(),
        "cw": np.ascontiguousarray(cwh),
        "subg": np.asarray(subln_g, np.float32).reshape(128),
        "lnfm": np.ascontiguousarray(np.concatenate([np.asarray(a, np.float32).reshape(8, 128).T for a in (in_ln_g, in_ln_b)], axis=1)),
        "lamv": np.concatenate([np.asarray(a, np.float32).reshape(64) for a in (lam_q1, lam_k1, lam_q2, lam_k2)]),
    }


def kernel(x_prompt, x_sample, mem_prompt, mem_sample, in_ln_g, in_ln_b, w_in, w_mem_kv,
           lam_q1, lam_k1, lam_q2, lam_k2, subln_g, conv_w, w_o, ln_g, ln_b):
    x_prompt = np.asarray(x_prompt, np.float32)
    x_sample = np.asarray(x_sample, np.float32)
    nb_p = x_prompt.shape[0]
    x_all = np.concatenate([x_prompt, x_sample], axis=0)
    mem_all = np.concatenate([np.asarray(mem_prompt, np.float32), np.asarray(mem_sample, np.float32)], axis=0)
    consts = _consts(in_ln_g, in_ln_b, w_in, w_mem_kv, lam_q1, lam_k1, lam_q2, lam_k2,
                     subln_g, conv_w, w_o, ln_g, ln_b)
    y = _run(x_all, mem_all, consts, NCORES, SEQ_PER_CORE)
    return (np.ascontiguousarray(y[:nb_p]), np.ascontiguousarray(y[nb_p:]))
```

```python
import contextlib
import math
import numpy as np
import concourse.bass as bass
import concourse.mybir as mybir
from concourse.bass_utils import run_bass_kernel_spmd

F32 = mybir.dt.float32
BF16 = mybir.dt.bfloat16
AF = mybir.ActivationFunctionType
ALU = mybir.AluOpType
AX = mybir.AxisListType

D = 1024
S = 2048
NT = 16
NB = 4
NMEM = 256
NCORES = 8
SEQ_PER_CORE = 6
NG = 12
LN_EPS = 1e-5
ALPHA = 2.0 ** 0.25
LAM_INIT = 0.2
ROPE_THETA = 500000.0
CELL = 64


class Op:
    __slots__ = ("eng", "fn", "deps", "idx", "sig", "cnt", "sem", "dma")

    def __init__(self, eng, fn, sem):
        self.eng = eng
        self.fn = fn
        self.deps = set()
        self.sig = False
        self.cnt = 0
        self.sem = sem
        self.dma = sem is not None


class Sched:
    def __init__(self):
        self.engs = {"pe": [], "act": [], "dve": [], "pool": [], "sp": []}
        self.cells = {}
        self.dma_counts = {}

    @staticmethod
    def _cells_of(ap):
        name = ap.tensor.name
        if name not in ("arena", "ps"):
            return ()
        esz = 2 if ap.dtype == BF16 else 4
        pairs = ap.ap
        row = pairs[0][0]
        start = int(ap.offset) % row if row > 0 else int(ap.offset)
        ext = 1
        for st, cn in pairs[1:]:
            ext += (cn - 1) * st
        b0 = start * esz
        b1 = (start + ext) * esz
        c0 = b0 // (CELL * 4)
        c1 = (b1 - 1) // (CELL * 4)
        return [(name, c) for c in range(c0, c1 + 1)]

    def _dep(self, o, p, kind):
        if p is o:
            return
        if p.dma:
            o.deps.add(p)
            return
        if p.eng == o.eng:
            if o.dma:
                o.deps.add(p)
                return
            if o.eng == "pe":
                return
            o.deps.add(p)
            return
        o.deps.add(p)

    def op(self, eng, fn, outs=(), ins=(), xr=(), xw=(), sem=None):
        o = Op(eng, fn, sem)
        o.idx = len(self.engs[eng])
        self.engs[eng].append(o)
        rc = list(xr)
        wc = list(xw)
        for a in ins:
            rc.extend(self._cells_of(a))
        for a in outs:
            wc.extend(self._cells_of(a))
        cells = self.cells
        for r in rc:
            c = cells.get(r)
            if c is not None and c[0] is not None:
                self._dep(o, c[0], "raw")
        for r in wc:
            c = cells.get(r)
            if c is not None:
                if c[0] is not None:
                    self._dep(o, c[0], "waw")
                for rd in c[1].values():
                    self._dep(o, rd, "war")
        key = ("dma", id(o)) if o.dma else eng
        for r in rc:
            c = cells.get(r)
            if c is None:
                c = cells[r] = [None, {}]
            c[1][key] = o
        for r in wc:
            cells[r] = [o, {}]
        if o.dma:
            n = self.dma_counts.get(sem, 0) + 16
            self.dma_counts[sem] = n
            o.cnt = n
        return o

    def finalize(self):
        for ops in self.engs.values():
            for o in ops:
                for p in o.deps:
                    if not p.dma:
                        p.sig = True
        for ops in self.engs.values():
            n = 0
            for o in ops:
                if not o.dma and o.sig:
                    n += 1
                    o.cnt = n

    def emit(self, eng_name, eng, sems):
        waited = {}
        for o in self.engs[eng_name]:
            need = {}
            for p in o.deps:
                key = p.sem if p.dma else ("eng", p.eng)
                if waited.get(key, 0) >= p.cnt:
                    continue
                if need.get(key, 0) < p.cnt:
                    need[key] = p.cnt
            for key, cnt in need.items():
                eng.wait_ge(sems[key], cnt)
                waited[key] = cnt
            inst = o.fn(eng)
            if o.dma:
                inst.then_inc(sems[o.sem], 16)
            elif o.sig:
                inst.then_inc(sems[("eng", o.eng)], 1)


def build_program(nseq):
    nc = bass.Bass("TRN2", target_bir_lowering=False)
    x_d = nc.dram_tensor("x", [nseq, S, D], F32, kind="ExternalInput").ap()
    mem_d = nc.dram_tensor("mem", [nseq, NMEM, D], F32, kind="ExternalInput").ap()
    wsrc_d = nc.dram_tensor("wsrc", [NG, 128, 4096], F32, kind="ExternalInput").ap()
    lnp_d = nc.dram_tensor("lnp", [4, D], F32, kind="ExternalInput").ap()
    ctab_d = nc.dram_tensor("ctab", [128, S], F32, kind="ExternalInput").ap()
    stab_d = nc.dram_tensor("stab", [128, S], F32, kind="ExternalInput").ap()
    ident_d = nc.dram_tensor("ident", [128, 128], F32, kind="ExternalInput").ap()
    permm_d = nc.dram_tensor("permm", [128, 128], F32, kind="ExternalInput").ap()
    cw_d = nc.dram_tensor("cw", [128, 6], F32, kind="ExternalInput").ap()
    sg_d = nc.dram_tensor("subg", [128], F32, kind="ExternalInput").ap()
    lam_d = nc.dram_tensor("lamv", [256], F32, kind="ExternalInput").ap()
    lnfm_d = nc.dram_tensor("lnfm", [128, 16], F32, kind="ExternalInput").ap()
    y_d = nc.dram_tensor("y", [nseq, S, D], F32, kind="ExternalOutput").ap()
    wsc_d = nc.dram_tensor("wsc", [NG, 128, 4096], BF16, kind="Internal").ap()
    rsc_d = nc.dram_tensor("rsc", [2, 512], F32, kind="Internal").ap()

    off = [0]

    def alloc(words):
        words = (words + CELL - 1) // CELL * CELL
        o = off[0]
        off[0] += words
        return o

    o_gin, o_bin, o_g2, o_b2 = alloc(1024), alloc(1024), alloc(1024), alloc(1024)
    o_ctab, o_stab = alloc(2048), alloc(2048)
    o_idb = alloc(64)
    o_ones = alloc(64)
    o_permb = alloc(64)
    o_sgp1 = alloc(1)
    o_lnfm = alloc(16)
    o_sgp = alloc(128)
    o_cw = alloc(8)
    o_lamv = alloc(256)
    o_lprod = alloc(128)
    o_ls = alloc(2)
    o_le = alloc(2)
    o_nl = alloc(1)
    o_stats = alloc(32)
    o_st12 = alloc(12)
    o_mv = alloc(2)
    o_st12b = alloc(12)
    o_mvb = alloc(2)
    o_st2 = alloc(2)
    o_rs = alloc(8)
    o_rsl = alloc(4)
    o_ss = alloc(4)
    o_rinv = alloc(4)
    o_rsm = alloc(16)
    o_mhalf = alloc(4)
    o_smin = alloc(4)
    o_smout = alloc(4)
    o_epsc = alloc(4)
    o_eps128 = alloc(4)
    o_tmpv = alloc(4)
    o_wg = [alloc(2048), alloc(2048)]
    o_QT = alloc(4096)
    o_KT = alloc(4096)
    o_KT1 = alloc(4096)
    o_VA = alloc(4160)
    o_gate = alloc(4096)
    o_ocv = alloc(2048)
    o_omx = alloc(2048)
    o_kmx = alloc(256)
    o_vmx = alloc(264)
    shared0 = off[0]
    o_hT = alloc(8192)
    tmp0 = off[0]
    o_xt = [alloc(1024), alloc(1024), alloc(1024)]
    o_hb = [alloc(512) for _ in range(4)]
    endA = off[0]
    off[0] = tmp0
    o_memb = alloc(1024)
    o_memT = alloc(1024)
    endM = off[0]
    off[0] = tmp0
    o_tA, o_tB = alloc(512), alloc(512)
    o_qtmp = [alloc(256) for _ in range(4)]
    endR = off[0]
    off[0] = tmp0
    o_csb, o_sgc, o_bgb, o_u, o_yc = alloc(512), alloc(512), alloc(512), alloc(2052), alloc(512)
    endC = off[0]
    off[0] = tmp0
    o_gtmp = alloc(512)
    off[0] = tmp0
    o_qmxb = alloc(512)
    o_gmxb = alloc(1024)
    o_Em = [alloc(256) for _ in range(4)]
    o_omxf = alloc(1024)
    o_ofm = alloc(512)
    endG8 = off[0]
    off[0] = shared0
    o_E = [alloc(256) for _ in range(4)]
    o_mix = [alloc(1024), alloc(1024)]
    o_rcp = [alloc(512), alloc(512)]
    o_tO = [alloc(512), alloc(512)]
    o_Of = alloc(512)
    o_rscr = alloc(512)
    o_pp = alloc(512)
    o_sqb = alloc(256)
    o_xr = [alloc(1024) for _ in range(4)]
    endP = off[0]
    AW = max(endA, endM, endR, endC, endG8, endP)

    arena = nc.alloc_sbuf_tensor("arena", [128, AW], F32)
    ps = nc.alloc_psum_tensor("ps", [128, 4096], F32)

    def f32v(o, n):
        return arena[:, o:o + n]

    def bfv(o, nwords):
        return arena[:, o:o + nwords].bitcast(BF16)

    def bank(b, n=512):
        return ps[:, b * 512:b * 512 + n]

    def bank_bf(b):
        return ps[:, b * 512:(b + 1) * 512].bitcast(BF16)

    gin, bin_, g2, b2 = f32v(o_gin, 1024), f32v(o_bin, 1024), f32v(o_g2, 1024), f32v(o_b2, 1024)
    ctab, stab = f32v(o_ctab, 2048), f32v(o_stab, 2048)
    idb = bfv(o_idb, 64)
    onesb = bfv(o_ones, 64)
    permb = bfv(o_permb, 64)
    sgp1 = f32v(o_sgp1, 1)
    lnfm = f32v(o_lnfm, 16)
    sgp = f32v(o_sgp, 128)
    cw = f32v(o_cw, 6)
    lamv = f32v(o_lamv, 256)
    lprod = f32v(o_lprod, 128)
    ls, le, nl = f32v(o_ls, 2), f32v(o_le, 2), f32v(o_nl, 1)
    stats = f32v(o_stats, 32).rearrange("p (t k) -> p t k", k=2)
    st12, mv = f32v(o_st12, 12), f32v(o_mv, 2)
    st12b, mvb, st2 = f32v(o_st12b, 12), f32v(o_mvb, 2), f32v(o_st2, 2)
    rs = f32v(o_rs, 8).rearrange("p (q c) -> p q c", c=2)
    rsl, ss, rinv = f32v(o_rsl, 4), f32v(o_ss, 4), f32v(o_rinv, 4)
    rsm = f32v(o_rsm, 16).rearrange("p (q h) -> p q h", h=4)
    sm_in, sm_out = f32v(o_smin, 4), f32v(o_smout, 4)
    mhalf, epsc, eps128, tmpv = f32v(o_mhalf, 4), f32v(o_epsc, 4), f32v(o_eps128, 4), f32v(o_tmpv, 4)
    wg = [bfv(o, 2048).rearrange("p (c n) -> p c n", n=512) for o in o_wg]
    wg_flat = [bfv(o, 2048) for o in o_wg]
    QT = bfv(o_QT, 4096).rearrange("p (h t) -> p h t", t=S)
    KT = bfv(o_KT, 4096).rearrange("p (h t) -> p h t", t=S)
    KT1 = bfv(o_KT1, 4096).rearrange("p (h t) -> p h t", t=S)
    VA = bfv(o_VA, 4160).rearrange("p (t h e) -> p t h e", h=4, e=130)
    gateT = bfv(o_gate, 4096).rearrange("p (h t) -> p h t", t=S)
    ocvT = bfv(o_ocv, 2048).rearrange("p (i t) -> p i t", t=S)
    omxT = bfv(o_omx, 2048).rearrange("p (i t) -> p i t", t=S)
    KmxT = bfv(o_kmx, 256).rearrange("p (i t) -> p i t", t=256)
    Vmx = bfv(o_vmx, 264).rearrange("p (t h e) -> p t h e", h=4, e=66)
    hT = bfv(o_hT, 8192).rearrange("p (b c t) -> p b c t", c=8, t=512)
    xt = [f32v(o, 1024) for o in o_xt]
    hb = [bfv(o, 512) for o in o_hb]
    memb = bfv(o_memb, 1024).rearrange("p (t d) -> p t d", d=1024)
    memT = bfv(o_memT, 1024).rearrange("p (c t) -> p c t", t=256)
    tA, tB = f32v(o_tA, 512), f32v(o_tB, 512)
    qtmp = [bfv(o, 256) for o in o_qtmp]
    csb, sgc, bgb, yc = f32v(o_csb, 512), f32v(o_sgc, 512), f32v(o_bgb, 512), f32v(o_yc, 512)
    u = f32v(o_u, 2050)
    gtmp = f32v(o_gtmp, 512)
    qmxb = bfv(o_qmxb, 512).rearrange("p (i t) -> p i t", t=512)
    gmxb = f32v(o_gmxb, 1024).rearrange("p (q n) -> p q n", n=256)
    Em = [bfv(o, 256) for o in o_Em]
    omxf = f32v(o_omxf, 1024)
    ofm = bfv(o_ofm, 512).rearrange("p (q n) -> p q n", n=256)
    E = [bfv(o, 256) for o in o_E]
    mix = [bfv(o, 1024).rearrange("p (h t) -> p h t", t=512) for o in o_mix]
    rcp = [f32v(o, 512) for o in o_rcp]
    tO = [f32v(o, 512) for o in o_tO]
    Of = f32v(o_Of, 512)
    rscr = f32v(o_rscr, 512)
    pp = f32v(o_pp, 512)
    sqb = bfv(o_sqb, 256)
    xr = [f32v(o, 1024) for o in o_xr]

    sc = Sched()
    semkeys = [("eng", e) for e in ("pe", "act", "dve", "pool")]
    dma_sem_names = set()

    def dsem(name):
        dma_sem_names.add(name)
        return name

    def dma_sp(out, in_, sem, xr_=(), xw_=()):
        sc.op("sp", lambda e: e.dma_start(out=out, in_=in_), outs=[out], ins=[in_],
              xr=xr_, xw=xw_, sem=dsem(sem))

    def dma_pool(out, in_, sem, xr_=(), xw_=()):
        sc.op("pool", lambda e: e.dma_start(out=out, in_=in_), outs=[out], ins=[in_],
              xr=xr_, xw=xw_, sem=dsem(sem))

    def mm(out, lhsT, rhs, start, stop, skip=False):
        sc.op("pe", lambda e: e.matmul(out, lhsT, rhs, start=start, stop=stop, skip_group_check=skip),
              outs=[out], ins=[lhsT, rhs])

    def tr(out, in_):
        sc.op("pe", lambda e: e.transpose(out, in_, idb), outs=[out], ins=[in_, idb])

    def act(out, in_, func, bias=0.0, scale=1.0, extra=()):
        ins = [in_] + [a for a in (bias, scale) if not isinstance(a, float)] + list(extra)
        sc.op("act", lambda e: e.activation(out, in_, func, bias=bias, scale=scale),
              outs=[out], ins=ins)

    def tt(eng, out, in0, in1, op, extra=()):
        sc.op(eng, lambda e: e.tensor_tensor(out, in0, in1, op), outs=[out], ins=[in0, in1] + list(extra))

    def ts(eng, out, in0, s1, s2, op0, op1=None):
        ins = [in0] + [a for a in (s1, s2) if a is not None and not isinstance(a, float)]
        if op1 is None:
            sc.op(eng, lambda e: e.tensor_scalar(out, in0, s1, s2, op0), outs=[out], ins=ins)
        else:
            sc.op(eng, lambda e: e.tensor_scalar(out, in0, s1, s2, op0, op1), outs=[out], ins=ins)

    def stt(eng, out, in0, scalar, in1, op0, op1):
        ins = [in0, in1] + ([scalar] if not isinstance(scalar, float) else [])
        sc.op(eng, lambda e: e.scalar_tensor_tensor(out, in0, scalar, in1, op0, op1),
              outs=[out], ins=ins)

    def cp(eng, out, in_):
        if eng == "act":
            act(out, in_, AF.Copy)
        else:
            sc.op(eng, lambda e: e.tensor_copy(out, in_), outs=[out], ins=[in_])

    def memset(eng, out, val):
        sc.op(eng, lambda e: e.memset(out, val), outs=[out])

    def bc(ap2d, shape):
        return ap2d.unsqueeze(len(ap2d.shape)).broadcast_to(shape)

    for i, dst in enumerate((gin, bin_, g2, b2)):
        dma_sp(dst, lnp_d[i].partition_broadcast(128), "c%d" % i)
    dma_sp(ctab, ctab_d, "c4")
    dma_sp(stab, stab_d, "c5")
    dma_sp(cw, cw_d, "c6")
    dma_sp(sgp, sg_d.partition_broadcast(128), "c7")
    dma_sp(sgp1, sg_d.rearrange("(p o) -> p o", o=1), "c10")
    dma_sp(lnfm, lnfm_d, "c11")
    dma_sp(lamv, lam_d.partition_broadcast(128), "c8")
    dma_pool(idb, ident_d, "c9")
    dma_pool(permb, permm_d, "c12")
    ts("dve", sgp1, sgp1, (1.0 - LAM_INIT) * math.sqrt(128.0), None, ALU.mult)
    memset("pool", onesb, 1.0)
    lv = lamv.rearrange("p (a b d) -> p a b d", a=2, b=2)
    tt("dve", lprod.rearrange("p (a d) -> p a d", d=64), lv[:, :, 0, :], lv[:, :, 1, :], ALU.mult)
    sc.op("dve", lambda e: e.tensor_reduce(ls, lprod.rearrange("p (a d) -> p a d", d=64), AX.X, ALU.add),
          outs=[ls], ins=[lprod])
    act(le, ls, AF.Exp)
    stt("dve", nl, le[:, 0:1], -1.0, le[:, 1:2], ALU.mult, ALU.add)
    ts("dve", nl, nl, -LAM_INIT, None, ALU.add)
    memset("pool", mhalf, -0.5)
    memset("pool", epsc, LN_EPS)
    memset("pool", eps128, 128.0 * LN_EPS)
    memset("pool", KT[64:128, :, :], 0.0)
    memset("pool", KT1[0:64, :, :], 0.0)
    memset("pool", VA[:, :, :, 128:130], 1.0)
    memset("pool", Vmx[:, :, :, 64:66], 1.0)
    for g in range(NG):
        sl = g % 2
        dma_pool(wg_flat[sl], wsrc_d[g], "wc%d" % sl)
        dma_sp(wsc_d[g], wg_flat[sl], "ws%d" % sl, xw_=[("wsc", g)])

    wslot = [0]

    def load_w(g):
        sl = wslot[0] % 2
        wslot[0] += 1
        dma_sp(wg_flat[sl], wsc_d[g], "wl%d" % sl, xr_=[("wsc", g)])
        return sl

    psrot = [0]

    def next_pair():
        p = psrot[0] % 2
        psrot[0] += 1
        return 2 * p, 2 * p + 1

    def fm_tile(bk, sl, i, blk):
        for c in range(8):
            mm(bank(bk), wg[sl][:, c, i * 128:(i + 1) * 128], hT[:, blk, c, :], c == 0, c == 7)

    def ln_stats(src, st_, mv_, rstd_out, nmr_out):
        sc.op("dve", lambda e: e.bn_stats(st_[:, 0:6], src[:, 0:512]), outs=[st_[:, 0:6]], ins=[src[:, 0:512]])
        sc.op("dve", lambda e: e.bn_stats(st_[:, 6:12], src[:, 512:1024]), outs=[st_[:, 6:12]], ins=[src[:, 512:1024]])
        sc.op("dve", lambda e: e.bn_aggr(mv_, st_), outs=[mv_], ins=[st_])
        tt("pool", tmpv[:, 0:1], mv_[:, 1:2], epsc[:, 0:1], ALU.add)
        tt("pool", rstd_out, tmpv[:, 0:1], mhalf[:, 0:1], ALU.pow)
        stt("dve", nmr_out, mv_[:, 0:1], -1.0, rstd_out, ALU.mult, ALU.mult)

    ystores = {}
    ACC = ps[:, 4 * 512:8 * 512].rearrange("p (q n) -> p q n", n=512)

    for s in range(nseq):
        slkv = load_w(9)
        dma_pool(memb, mem_d[s].rearrange("(t p) d -> p t d", p=128), "mem")
        for t in range(2):
            pt = bank_bf(7)
            for c in range(8):
                tr(pt[:, c * 128:(c + 1) * 128], memb[:, t, c * 128:(c + 1) * 128])
            cp("act", memT[:, :, t * 128:(t + 1) * 128], pt.rearrange("p (c t) -> p c t", t=128))
        for i in range(2):
            bk = next_pair()[0]
            for c in range(8):
                mm(bank(bk, 256), wg[slkv][:, c, i * 128:(i + 1) * 128], memT[:, c, :], c == 0, c == 7)
            cp("act", KmxT[:, i, :], bank(bk, 256))
        for t in range(2):
            bk = next_pair()[0]
            for c in range(8):
                mm(bank(bk, 256), memT[:, c, t * 128:(t + 1) * 128], wg[slkv][:, c, 256:512], c == 0, c == 7)
            cp("dve", Vmx[:, t, :, 0:64], bank(bk, 256).rearrange("p (h e) -> p h e", e=64))

        sl_next = load_w(0)
        def a_front(p):
            for k in range(2):
                t = 2 * p + k
                xs = xt[t % 3]
                dma_sp(xs, x_d[s, t * 128:(t + 1) * 128, :], "x%d" % (t % 3))
                ln_stats(xs, st12, mv, stats[:, t, 0:1], stats[:, t, 1:2])
                act(hb[(p % 2) * 2 + k], xs, AF.Identity, bias=stats[:, t, 1:2], scale=stats[:, t, 0:1])

        def a_region(p):
            b0 = 4 + 2 * (p % 2)
            return b0, ps[:, b0 * 512:(b0 + 2) * 512].bitcast(BF16)

        def a_mid(p):
            b0, region = a_region(p)
            for c in range(8):
                for k in range(2):
                    tr(region[:, c * 256 + k * 128:c * 256 + (k + 1) * 128],
                       hb[(p % 2) * 2 + k][:, c * 128:(c + 1) * 128])

        def a_back(p):
            b0, region = a_region(p)
            t0 = 2 * p
            col0 = (t0 % 4) * 128
            for c in range(8):
                act(hT[:, t0 // 4, c, col0:col0 + 256], region[:, c * 256:(c + 1) * 256], AF.Identity,
                    bias=lnfm[:, 8 + c:9 + c], scale=lnfm[:, c:c + 1], extra=[bank_bf(b0 + c // 4)])

        npair = NT // 2
        a_front(0)
        a_mid(0)
        for p in range(1, npair):
            a_front(p)
            a_mid(p)
            a_back(p - 1)
        a_back(npair - 1)

        for g in range(9):
            sl = sl_next
            if g < 8:
                sl_next = load_w(g + 1)
            if g < 4:
                h = g
                for blk in range(NB + 1):
                    if blk < NB:
                        sb = 4 * (blk % 2)
                        fm_tile(sb, sl, 0, blk)
                        fm_tile(sb + 2, sl, 2, blk)
                        cp("act", qtmp[2 * (blk % 2)], bank(sb))
                        cp("act", qtmp[2 * (blk % 2) + 1], bank(sb + 2))
                    if blk >= 1:
                        pb_ = blk - 1
                        sb = 4 * (pb_ % 2)
                        cs = slice(pb_ * 512, (pb_ + 1) * 512)
                        mm(bank(sb + 1), permb, qtmp[2 * (pb_ % 2)], True, True)
                        mm(bank(sb + 3), permb, qtmp[2 * (pb_ % 2) + 1], True, True)
                        tt("dve", tA, bank(sb), ctab[:, cs], ALU.mult, extra=[qtmp[2 * (pb_ % 2)]])
                        tt("dve", tB, bank(sb + 1), stab[:, cs], ALU.mult)
                        tt("dve", QT[:, h, cs], tA, tB, ALU.add)
                        tt("dve", tA, bank(sb + 2), ctab[:, cs], ALU.mult, extra=[qtmp[2 * (pb_ % 2) + 1]])
                        tt("dve", tB, bank(sb + 3), stab[:, cs], ALU.mult)
                        tt("dve", KT[0:64, h, cs], tA[0:64, :], tB[0:64, :], ALU.add)
                        tt("dve", KT1[64:128, h, cs], tA[64:128, :], tB[64:128, :], ALU.add)
            elif g < 6:
                i = g - 4
                memset("pool", u[:, 0:1], 0.0)
                memset("pool", u[:, 2049:2050], 0.0)
                for blk in range(NB):
                    ba, bb = next_pair()
                    fm_tile(ba, sl, 0, blk)
                    cp("act", csb, bank(ba))
                    fm_tile(bb, sl, 1, blk)
                    tt("dve", u[:, 1 + blk * 512:1 + (blk + 1) * 512], bank(bb), csb, ALU.mult)
                for blk in range(NB):
                    ba, bb = next_pair()
                    fm_tile(ba, sl, 2, blk)
                    act(sgc, bank(ba), AF.Silu)
                    fm_tile(bb, sl, 3, blk)
                    tt("dve", bgb, bank(bb), sgc, ALU.mult)
                    b0 = blk * 512
                    ts("dve", yc, u[:, b0:b0 + 512], cw[:, 3 * i:3 * i + 1], None, ALU.mult)
                    stt("dve", yc, u[:, b0 + 1:b0 + 513], cw[:, 3 * i + 1:3 * i + 2], yc, ALU.mult, ALU.add)
                    stt("dve", yc, u[:, b0 + 2:b0 + 514], cw[:, 3 * i + 2:3 * i + 3], yc, ALU.mult, ALU.add)
                    tt("dve", ocvT[:, i, b0:b0 + 512], yc, bgb, ALU.mult)
            elif g == 6:
                for t in range(NT):
                    bk = next_pair()[t % 2]
                    for c in range(8):
                        mm(bank(bk), hT[:, t // 4, c, (t % 4) * 128:(t % 4 + 1) * 128], wg[sl][:, c, :], c == 0, c == 7)
                    cp("act", VA[:, t, :, 0:128], bank(bk).rearrange("p (h e) -> p h e", e=128))
            elif g == 7:
                for i in range(4):
                    for blk in range(NB):
                        bk = next_pair()[blk % 2]
                        fm_tile(bk, sl, i, blk)
                        act(gateT[:, i, blk * 512:(blk + 1) * 512], bank(bk), AF.Silu)
            else:
                for blk in range(NB):
                    for i in range(2):
                        bk = next_pair()[i]
                        fm_tile(bk, sl, i, blk)
                        cp("dve", qmxb[:, i, :], bank(bk))
                    for q in range(4):
                        bk = next_pair()[q % 2]
                        for c in range(8):
                            mm(bank(bk, 256), hT[:, blk, c, q * 128:(q + 1) * 128], wg[sl][:, c, 256:512], c == 0, c == 7)
                        act(gmxb[:, q, :], bank(bk, 256), AF.Silu)
                    its = [(p, mk) for p in range(2) for mk in range(2)]

                    def mx_s(n):
                        p, mk = its[n]
                        b0 = 2 * (n % 2)
                        mm(bank(b0), KmxT[0:64, p, mk * 128:(mk + 1) * 128], qmxb[0:64, p, :], True, True)
                        mm(bank(b0 + 1), KmxT[64:128, p, mk * 128:(mk + 1) * 128], qmxb[64:128, p, :], True, True)

                    mx_s(0)
                    for n, (p, mk) in enumerate(its):
                        if n + 1 < len(its):
                            mx_s(n + 1)
                        b0 = 2 * (n % 2)
                        act(Em[b0], bank(b0), AF.Exp, scale=0.125)
                        act(Em[b0 + 1], bank(b0 + 1), AF.Exp, scale=0.125)
                        for q in range(4):
                            for e in range(2):
                                hd = 2 * p + e
                                mm(ACC[:, q, hd * 66:hd * 66 + 65], Em[b0 + e][:, q * 128:(q + 1) * 128],
                                   Vmx[:, mk, hd, 0:65], mk == 0 and hd == 0, mk == 1, skip=True)
                    accm = ACC[:, :, 0:264].rearrange("p q (h e) -> p q h e", e=66)
                    sc.op("dve", lambda e: e.reciprocal(rsm, accm[:, :, :, 64]), outs=[rsm], ins=[accm[:, :, :, 64]])
                    for q in range(4):
                        tt("dve", omxf[:, q * 256:(q + 1) * 256].rearrange("p (h e) -> p h e", e=64),
                           accm[:, q, :, 0:64], bc(rsm[:, q, :], [128, 4, 64]), ALU.mult)
                    tt("dve", ofm, omxf.rearrange("p (q n) -> p q n", n=256), gmxb, ALU.mult)
                    pt = bank_bf(2)
                    for i in range(2):
                        for q in range(4):
                            tr(pt[:, (i * 4 + q) * 128:(i * 4 + q + 1) * 128], ofm[:, q, i * 128:(i + 1) * 128])
                    cp("dve", omxT[:, :, blk * 512:(blk + 1) * 512], pt.rearrange("p (i t) -> p i t", t=512))

        steps = [(j, h, kt, hf) for j in range(NB) for h in range(4) for kt in range(NT) for hf in range(2)]
        SPH = 2 * NT

        def s_step(i):
            j, h, kt, hf = steps[i]
            ks = slice(kt * 128, (kt + 1) * 128)
            q0 = j * 512 + hf * 256
            r = i % 3
            mm(ps[:, r * 512:r * 512 + 256], KT[:, h, ks], QT[:, h, q0:q0 + 256], True, True)
            mm(ps[:, r * 512 + 256:r * 512 + 512], KT1[:, h, ks], QT[:, h, q0:q0 + 256], True, True)

        def exp_step(i):
            act(E[i % 4], bank(i % 3), AF.Exp, scale=0.125)

        def av_step(i):
            j, h, kt, hf = steps[i]
            mm(bank(4 + hf), VA[:, kt, h, 0:128], E[i % 4], kt == 0, kt == NT - 1)
            mm(bank(6 + hf), onesb, E[i % 4], kt == 0, kt == NT - 1)

        def epi1(j, h):
            for hf in range(2):
                cp("dve", tO[hf], bank(4 + hf))
                cp("act", rcp[hf], bank(6 + hf))

        def epi2(j, h):
            for hf in range(2):
                tt("dve", tO[hf][:, 0:256], tO[hf][:, 0:256], rcp[hf][:, 256:512], ALU.mult)
                tt("dve", tO[hf][:, 256:512], tO[hf][:, 256:512], rcp[hf][:, 0:256], ALU.mult)
            for hf in range(2):
                hs = slice(hf * 256, (hf + 1) * 256)
                stt("dve", Of[:, hs], tO[hf][:, 256:512], nl[:, 0:1], tO[hf][:, 0:256], ALU.mult, ALU.add)
                tt("dve", pp[:, hs], rcp[hf][:, 0:256], rcp[hf][:, 256:512], ALU.mult)
            stt("dve", pp, pp, 128.0 * LN_EPS, pp, ALU.mult, ALU.mult)
            tt("dve", sqb, Of, Of, ALU.mult)

        def epi2m(j, h):
            mm(bank(3), onesb, sqb, True, True)
            tt("dve", rscr, bank(3), pp, ALU.add)
            dma_sp(rsc_d[0:1, :], rscr[0:1, :], "r0", xw_=[("rsc", 0)])
            dma_sp(sm_in, rsc_d[0].rearrange("(p c) -> p c", c=4), "r1", xr_=[("rsc", 0)])
            tt("pool", sm_out, sm_in, mhalf, ALU.pow)
            dma_sp(rsc_d[1].rearrange("(p c) -> p c", c=4), sm_out, "r2", xw_=[("rsc", 1)])
            dma_sp(pp, rsc_d[1].partition_broadcast(128), "r3", xr_=[("rsc", 1)])

        def epi2b(j, h):
            tt("dve", rscr, pp, gateT[:, h, j * 512:(j + 1) * 512], ALU.mult)
            stt("dve", mix[j % 2][:, h, :], Of, sgp1[:, 0:1], rscr, ALU.mult, ALU.mult)

        def c4_pre(j, q):
            t = 4 * j + q
            xs = xr[q]
            dma_sp(xs, x_d[s, t * 128:(t + 1) * 128, :], "xr%d" % q)
            stt("dve", xs, xs, stats[:, t, 0:1], gin, ALU.mult, ALU.mult)
            stt("dve", xs, gin, stats[:, t, 1:2], xs, ALU.mult, ALU.add)
            tt("dve", xs, xs, bin_, ALU.add)

        def c4_half(j, q, half, bk=3):
            t = 4 * j + q
            xs = xr[q]
            mx = mix[j % 2]
            tsl = slice(t * 128, (t + 1) * 128)
            for c in range(8):
                if c < 4:
                    lhsT = mx[:, c, q * 128:(q + 1) * 128]
                elif c < 6:
                    lhsT = ocvT[:, c - 4, tsl]
                else:
                    lhsT = omxT[:, c - 6, tsl]
                mm(bank(bk), lhsT, wg[slo[half]][:, c, :], c == 0, c == 7)
            hs = slice(half * 512, (half + 1) * 512)
            stt("dve", xs[:, hs], xs[:, hs], ALPHA, bank(bk), ALU.mult, ALU.add)

        def c4_tail(j, q):
            t = 4 * j + q
            xs = xr[q]
            ln_stats(xs, st12b, mvb, st2[:, 0:1], st2[:, 1:2])
            stt("dve", xs, xs, st2[:, 0:1], g2, ALU.mult, ALU.mult)
            stt("dve", xs, g2, st2[:, 1:2], xs, ALU.mult, ALU.add)
            tt("dve", xs, xs, b2, ALU.add)
            dma_sp(y_d[s, t * 128:(t + 1) * 128, :], xs, "ys%d" % q)

        slo = [load_w(10), load_w(11)]
        for q in range(4):
            c4_pre(0, q)
        nsteps = len(steps)
        deferred = {}
        fifo = []
        pre_done = set((0, q) for q in range(4))

        def defer(at, fn):
            deferred.setdefault(at, []).append(fn)

        def mix_final(jb):
            for q in range(4):
                fifo.append((jb, q))

        def do_pre(jb, q):
            if jb < NB and (jb, q) not in pre_done:
                pre_done.add((jb, q))
                c4_pre(jb, q)

        cur = [None]
        s_step(0)
        s_step(1)
        for i in range(nsteps):
            j, h, kt, hf = steps[i]
            if i + 2 < nsteps:
                s_step(i + 2)
            exp_step(i)
            av_step(i)
            for fn in deferred.pop(i, []):
                fn()
            if kt == NT - 1 and hf == 1:
                epi1(j, h)
                defer(i + 5, lambda j=j, h=h: epi2(j, h))
                defer(i + 13, lambda j=j, h=h: epi2m(j, h))
                if h == 3:
                    defer(i + 34, lambda j=j, h=h: (epi2b(j, h), mix_final(j)))
                else:
                    defer(i + 34, lambda j=j, h=h: epi2b(j, h))
            if hf == 1 and kt == 7 and fifo:
                cur[0] = fifo.pop(0)
                c4_half(cur[0][0], cur[0][1], 0)
            elif hf == 1 and kt == 8 and cur[0] is not None:
                jb, q = cur[0]
                cur[0] = None
                c4_half(jb, q, 1)
                c4_tail(jb, q)
                defer(i + 20, lambda jb=jb, q=q: do_pre(jb + 1, q))
        keys = sorted(deferred)
        flush_bank = [0]

        def flush_tile(jb, q):
            do_pre(jb, q)
            b0 = flush_bank[0] % 3
            flush_bank[0] += 1
            c4_half(jb, q, 0, bk=b0)
            c4_half(jb, q, 1, bk=3)
            c4_tail(jb, q)
            do_pre(jb + 1, q)

        first = True
        for k in keys:
            for fn in deferred[k]:
                fn()
            if first:
                first = False
                while fifo:
                    flush_tile(*fifo.pop(0))
        while fifo:
            flush_tile(*fifo.pop(0))

    sc.finalize()
    final_waits = [(k, v) for k, v in sc.dma_counts.items() if k.startswith("ys") or k.startswith("ws")]

    with contextlib.ExitStack() as es:
        sems = {}
        for k in semkeys:
            sems[k] = es.enter_context(nc.semaphore("e_" + k[1]))
        for name in sorted(dma_sem_names):
            sems[name] = es.enter_context(nc.semaphore("d_" + name))
        es.enter_context(nc.allow_low_precision("bf16 matmul operands, fp32 accumulation"))
        block = es.enter_context(nc.Block())

        @block.tensor
        def _(e):
            sc.emit("pe", e, sems)

        @block.scalar
        def _(e):
            sc.emit("act", e, sems)

        @block.vector
        def _(e):
            sc.emit("dve", e, sems)

        @block.gpsimd
        def _(e):
            sc.emit("pool", e, sems)

        @block.sync
        def _(e):
            sc.emit("sp", e, sems)
            for k, v in final_waits:
                e.wait_ge(sems[k], v)
    return nc


def _weight_groups(w_in, w_mem_kv, w_o):
    w = np.asarray(w_in, np.float32)[0]
    cols = []
    for h in range(4):
        base = np.arange(128)
        c, d = base // 64, base % 64
        dsw = np.where(d < 8, d + 8, np.where(d < 16, d - 8, d))
        q = h * 128 + base
        qp = h * 128 + c * 64 + dsw
        cols.append(np.concatenate([q, qp, 512 + q, 512 + qp]))
    for i in range(2):
        r = np.arange(128) + 128 * i
        cols.append(np.concatenate([2304 + r, 2560 + r, 2816 + r, 2048 + r]))
    cols.append(np.arange(1024, 1536))
    cols.append(np.arange(1536, 2048))
    cols.append(np.arange(3072, 3584))
    mats = [w[:, c] for c in cols]
    mats.append(np.asarray(w_mem_kv, np.float32)[0])
    wo = np.asarray(w_o, np.float32)[0]
    mats.append(wo[:, 0:512])
    mats.append(wo[:, 512:1024])
    out = np.empty((NG, 128, 8, 512), np.float32)
    for g, m in enumerate(mats):
        out[g] = m.reshape(8, 128, 512).transpose(1, 0, 2)
    return out.reshape(NG, 128, 4096)


def _perm_matrix():
    p = np.zeros((128, 128), np.float32)
    for r in range(128):
        d = r % 64
        k = r + 8 if d < 8 else (r - 8 if d < 16 else r)
        p[k, r] = 1.0
    return p


def _rope_tables():
    inv_freq = np.float64(ROPE_THETA) ** (-np.arange(0, 16, 2, dtype=np.float64) / 16.0)
    ang = np.arange(S, dtype=np.float64)[None, :] * inv_freq[:, None]
    cos, sin = np.cos(ang).astype(np.float32), np.sin(ang).astype(np.float32)
    ct = np.ones((128, S), np.float32)
    st = np.zeros((128, S), np.float32)
    for r in range(128):
        d = r % 64
        if d < 16:
            ct[r] = cos[d % 8]
            st[r] = -sin[d % 8] if d < 8 else sin[d % 8]
    return ct, st


_PROG = {}


def _run(x_all, mem_all, consts, ncores, nseq):
    if nseq not in _PROG:
        _PROG[nseq] = build_program(nseq)
    nc = _PROG[nseq]
    in_maps = []
    for c in range(ncores):
        m = dict(consts)
        m["x"] = np.ascontiguousarray(x_all[c * nseq:(c + 1) * nseq])
        m["mem"] = np.ascontiguousarray(mem_all[c * nseq:(c + 1) * nseq])
        in_maps.append(m)
    res = run_bass_kernel_spmd(nc, in_maps, core_ids=list(range(ncores)))
    return np.concatenate([np.asarray(r["y"]) for r in res.results], axis=0)


def _consts(in_ln_g, in_ln_b, w_in, w_mem_kv, lam_q1, lam_k1, lam_q2, lam_k2, subln_g, conv_w, w_o, ln_g, ln_b):
    ct, st = _rope_tables()
    cwh = np.asarray(conv_w, np.float32)[0].T.reshape(2, 128, 3).transpose(1, 0, 2).reshape(128, 6)
    return {
        "wsrc": _weight_groups(w_in, w_mem_kv, w_o),
        "lnp": np.stack([np.asarray(a, np.float32).reshape(D) for a in (in_ln_g, in_ln_b, ln_g, ln_b)]),
        "ctab": ct, "stab": st,
        "ident": np.eye(128, dtype=np.float32),
        "permm": _perm_matrix(),
        "cw": np.ascontiguousarray(cwh),
        "subg": np.asarray(subln_g, np.float32).reshape(128),
        "lnfm": np.ascontiguousarray(np.concatenate([np.asarray(a, np.float32).reshape(8, 128).T for a in (in_ln_g, in_ln_b)], axis=1)),
        "lamv": np.concatenate([np.asarray(a, np.float32).reshape(64) for a in (lam_q1, lam_k1, lam_q2, lam_k2)]),
    }


def kernel(x_prompt, x_sample, mem_prompt, mem_sample, in_ln_g, in_ln_b, w_in, w_mem_kv,
           lam_q1, lam_k1, lam_q2, lam_k2, subln_g, conv_w, w_o, ln_g, ln_b):
    x_prompt = np.asarray(x_prompt, np.float32)
    x_sample = np.asarray(x_sample, np.float32)
    nb_p = x_prompt.shape[0]
    x_all = np.concatenate([x_prompt, x_sample], axis=0)
    mem_all = np.concatenate([np.asarray(mem_prompt, np.float32), np.asarray(mem_sample, np.float32)], axis=0)
    consts = _consts(in_ln_g, in_ln_b, w_in, w_mem_kv, lam_q1, lam_k1, lam_q2, lam_k2,
                     subln_g, conv_w, w_o, ln_g, ln_b)
    y = _run(x_all, mem_all, consts, NCORES, SEQ_PER_CORE)
    return (np.ascontiguousarray(y[:nb_p]), np.ascontiguousarray(y[nb_p:]))
```
